# Optimizing a Trainium2 kernel written in Bass

```python
import jax, jax.numpy as jnp
from jax import lax
import numpy as np

D_MODEL = 1024
BATCH = 4
SEQ = 4096
DEPTH = 1
DEC_BATCH = 8
DEC_SEQ = 2048
PAST_LEN = 128

ATT_HEADS = 8
ATT_KV_HEADS = 2
ATT_HEAD_DIM = 64
ATT_WINDOW = 128
ATT_BLOCK = 128
ROPE_THETA = 500000.0
ROPE_DIM = ATT_HEAD_DIM // 4
DN_HEADS = 4
DN_HEAD_DIM = 128
DN_CONV = 5
DN_CHUNK = 64
X_HEADS = 4
X_HEAD_DIM = 128
MEM_LEN = 256
D_FF = 4 * D_MODEL
EPS = 1e-6

ATT_Q = ATT_HEADS * ATT_HEAD_DIM
ATT_KV = ATT_KV_HEADS * ATT_HEAD_DIM
DN_W = DN_HEADS * DN_HEAD_DIM
MIX_W = ATT_Q + DN_W
X_W = X_HEADS * X_HEAD_DIM
O_AQ = 0
O_AK = O_AQ + ATT_Q
O_AV = O_AK + ATT_KV
O_DQKV = O_AV + ATT_KV
O_DZ = O_DQKV + 3 * DN_W
O_BF = O_DZ + DN_W
O_BB = O_BF + DN_HEADS
O_AF = O_BB + DN_HEADS
O_AB = O_AF + DN_HEADS
IN_W = O_AB + DN_HEADS

kernel_name = "hymba_swa_gdn_sandwich_encoder"

F32 = jnp.float32


def _rms_norm(x, g):
    xf = x.astype(F32)
    y = xf * lax.rsqrt(jnp.mean(xf * xf, -1, keepdims=True) + EPS)
    return (y * g.astype(F32)).astype(x.dtype)


def _l2norm(t):
    return t * lax.rsqrt(jnp.sum(t * t, -1, keepdims=True) + EPS)


def _partial_rope(x, pos):
    half = ROPE_DIM // 2
    inv = ROPE_THETA ** (-(jnp.arange(half, dtype=F32) * 2.0 / ROPE_DIM))
    ang = pos.astype(F32)[:, None] * inv[None, :]
    cos = jnp.cos(ang)[None, :, None, :]
    sin = jnp.sin(ang)[None, :, None, :]
    xr = x[..., :ROPE_DIM].astype(F32)
    x1, x2 = xr[..., :half], xr[..., half:]
    rot = jnp.concatenate([x1 * cos - x2 * sin, x2 * cos + x1 * sin], -1).astype(x.dtype)
    return jnp.concatenate([rot, x[..., ROPE_DIM:]], -1)


def _window_attention(q, k, v, sink):
    B, T = q.shape[0], q.shape[1]
    nb = T // ATT_BLOCK
    G = ATT_HEADS // ATT_KV_HEADS
    qb = q.reshape(B, nb, ATT_BLOCK, ATT_KV_HEADS, G, ATT_HEAD_DIM)

    def band(t):
        tp = jnp.pad(t, ((0, 0), (ATT_BLOCK, ATT_BLOCK), (0, 0), (0, 0)))
        tb = tp.reshape(B, nb + 2, ATT_BLOCK, ATT_KV_HEADS, ATT_HEAD_DIM)
        return jnp.concatenate([tb[:, :-2], tb[:, 1:-1], tb[:, 2:]], axis=2)

    kb, vb = band(k), band(v)
    s = jnp.einsum('bnqhgd,bnkhd->bnhgqk', qb, kb).astype(F32) * (ATT_HEAD_DIM ** -0.5)
    qi = jnp.arange(ATT_BLOCK)[:, None]
    kj = jnp.arange(3 * ATT_BLOCK)[None, :]
    rel = kj - ATT_BLOCK - qi
    kpos = jnp.arange(nb)[:, None, None] * ATT_BLOCK - ATT_BLOCK + kj[None]
    valid = (jnp.abs(rel) <= ATT_WINDOW)[None] & (kpos >= 0) & (kpos < T)
    s = jnp.where(valid[None, :, None, None], s, -1e30)
    sk = sink.astype(F32).reshape(ATT_KV_HEADS, G)[None, None, :, :, None, None]
    m = jnp.maximum(jnp.max(s, -1, keepdims=True), sk)
    p = jnp.exp(s - m)
    p = p / (jnp.sum(p, -1, keepdims=True) + jnp.exp(sk - m))
    o = jnp.einsum('bnhgqk,bnkhd->bnqhgd', p.astype(v.dtype), vb)
    return o.reshape(B, T, ATT_Q)


def _centred_conv_silu(x, w):
    y = lax.conv_general_dilated(
        x, w[:, None, :].astype(x.dtype), window_strides=(1,),
        padding=[(DN_CONV // 2, DN_CONV // 2)],
        dimension_numbers=('NWC', 'WIO', 'NWC'), feature_group_count=x.shape[-1])
    return jax.nn.silu(y)


def _chunked_gated_delta(q, k, v, g, beta):
    B, T, H, Dk = q.shape
    Dv = v.shape[-1]
    C = DN_CHUNK
    N = T // C
    to_c = lambda t: jnp.moveaxis(t.reshape(B, N, C, H, -1), 3, 1)
    q = to_c(q) * (Dk ** -0.5)
    k = to_c(k)
    v = to_c(v)
    g = jnp.moveaxis(g.reshape(B, N, C, H), 3, 1)
    beta = jnp.moveaxis(beta.reshape(B, N, C, H), 3, 1)
    gc = jnp.cumsum(g, -1)
    tri = jnp.tril(jnp.ones((C, C), bool))
    strict = jnp.tril(jnp.ones((C, C), bool), -1)
    diff = gc[..., :, None] - gc[..., None, :]
    decay = jnp.where(tri, jnp.exp(jnp.where(tri, diff, 0.0)), 0.0)
    kb = k * beta[..., None]
    L = jnp.where(strict, jnp.einsum('bhncd,bhnsd->bhncs', kb, k) * decay, 0.0)
    A = L + jnp.eye(C, dtype=F32)
    rhs = jnp.concatenate([v * beta[..., None], kb * jnp.exp(gc)[..., None]], -1)
    sol = lax.linalg.triangular_solve(A, rhs, left_side=True, lower=True, unit_diagonal=True)
    u, w = sol[..., :Dv], sol[..., Dv:]
    attn = jnp.where(tri, jnp.einsum('bhncd,bhnsd->bhncs', q, k) * decay, 0.0)
    qg = q * jnp.exp(gc)[..., None]
    kd = k * jnp.exp(gc[..., -1:] - gc)[..., None]
    glast = jnp.exp(gc[..., -1])

    def step(S, inp):
        u_i, w_i, attn_i, qg_i, kd_i, gl_i = inp
        v_new = u_i - jnp.einsum('bhcd,bhde->bhce', w_i, S)
        o = jnp.einsum('bhcd,bhde->bhce', qg_i, S) + jnp.einsum('bhcs,bhse->bhce', attn_i, v_new)
        S = S * gl_i[..., None, None] + jnp.einsum('bhcd,bhce->bhde', kd_i, v_new)
        return S, o

    xs = tuple(jnp.moveaxis(t, 2, 0) for t in (u, w, attn, qg, kd, glast))
    S0 = jnp.zeros((B, H, Dk, Dv), F32)
    _, o = lax.scan(step, S0, xs)
    return o.transpose(1, 0, 3, 2, 4).reshape(B, T, H, Dv)


def _delta_direction(q, k, v, a, b, A_log, dt_bias, reverse):
    g = -jnp.exp(A_log.astype(F32)) * jax.nn.softplus(a.astype(F32) + dt_bias.astype(F32))
    beta = jax.nn.sigmoid(b.astype(F32))
    if reverse:
        q, k, v, g, beta = (jnp.flip(t, 1) for t in (q, k, v, g, beta))
    o = _chunked_gated_delta(q, k, v, g, beta)
    return jnp.flip(o, 1) if reverse else o


def _deltanet(qkv, z, bf, bb, af, ab, conv_w, A_log_f, A_log_b, dt_f, dt_b, norm_w):
    B, T = qkv.shape[0], qkv.shape[1]
    qkv = _centred_conv_silu(qkv, conv_w).astype(F32)
    q = _l2norm(qkv[..., :DN_W].reshape(B, T, DN_HEADS, DN_HEAD_DIM))
    k = _l2norm(qkv[..., DN_W:2 * DN_W].reshape(B, T, DN_HEADS, DN_HEAD_DIM))
    v = qkv[..., 2 * DN_W:].reshape(B, T, DN_HEADS, DN_HEAD_DIM)
    o = (_delta_direction(q, k, v, af, bf, A_log_f, dt_f, False)
         + _delta_direction(q, k, v, ab, bb, A_log_b, dt_b, True))
    o = o * lax.rsqrt(jnp.mean(o * o, -1, keepdims=True) + EPS) * norm_w.astype(F32)
    o = o * jax.nn.silu(z.astype(F32).reshape(B, T, DN_HEADS, DN_HEAD_DIM))
    return o.reshape(B, T, DN_W).astype(z.dtype)


def _cross_attention(h, mem, wq, wkv, wo, mem_g):
    B, T = h.shape[0], h.shape[1]
    mn = _rms_norm(mem, mem_g)
    q = (h @ wq).reshape(B, T, X_HEADS, X_HEAD_DIM)
    kv = mn @ wkv
    k = kv[..., :X_W].reshape(B, -1, X_HEADS, X_HEAD_DIM)
    v = kv[..., X_W:].reshape(B, -1, X_HEADS, X_HEAD_DIM)
    s = jnp.einsum('bqhd,bkhd->bhqk', q, k).astype(F32) * (X_HEAD_DIM ** -0.5)
    p = jax.nn.softmax(s, -1)
    o = jnp.einsum('bhqk,bkhd->bqhd', p.astype(v.dtype), v).reshape(B, T, X_W)
    return o @ wo


def _layer(x, mem, w_in, attn_sink, conv_w, A_log_f, A_log_b, dt_f, dt_b, dn_norm_w, w_out,
           xa_wq, xa_wkv, xa_wo, mem_g, w1, w2,
           g_pre_mix, g_post_mix, g_pre_xa, g_post_xa, g_pre_mlp, g_post_mlp):
    B, T = x.shape[0], x.shape[1]
    h = _rms_norm(x, g_pre_mix)
    proj = h @ w_in
    pos = jnp.arange(T)
    aq = _partial_rope(proj[..., O_AQ:O_AK].reshape(B, T, ATT_HEADS, ATT_HEAD_DIM), pos)
    ak = _partial_rope(proj[..., O_AK:O_AV].reshape(B, T, ATT_KV_HEADS, ATT_HEAD_DIM), pos)
    av = proj[..., O_AV:O_DQKV].reshape(B, T, ATT_KV_HEADS, ATT_HEAD_DIM)
    att = _window_attention(aq, ak, av, attn_sink)
    dn = _deltanet(proj[..., O_DQKV:O_DZ], proj[..., O_DZ:O_BF],
                   proj[..., O_BF:O_BB], proj[..., O_BB:O_AF],
                   proj[..., O_AF:O_AB], proj[..., O_AB:IN_W],
                   conv_w, A_log_f, A_log_b, dt_f, dt_b, dn_norm_w)
    mix = jnp.concatenate([att, dn], -1) @ w_out
    x = x + _rms_norm(mix, g_post_mix)
    h = _rms_norm(x, g_pre_xa)
    x = x + _rms_norm(_cross_attention(h, mem, xa_wq, xa_wkv, xa_wo, mem_g), g_post_xa)
    h = _rms_norm(x, g_pre_mlp)
    x = x + _rms_norm(jnp.square(jax.nn.relu(h @ w1)) @ w2, g_post_mlp)
    return x


def _trunk(x, mem, w_in, attn_sink, dn_conv_w, dn_A_log_f, dn_A_log_b, dn_dt_bias_f, dn_dt_bias_b,
           dn_norm_w, w_out, xa_wq, xa_wkv, xa_wo, mem_norm_g, mlp_w1, mlp_w2,
           norm_pre_mix, norm_post_mix, norm_pre_xa, norm_post_xa, norm_pre_mlp, norm_post_mlp):
    for l in range(DEPTH):
        x = _layer(x, mem, w_in[l], attn_sink[l], dn_conv_w[l], dn_A_log_f[l], dn_A_log_b[l],
                   dn_dt_bias_f[l], dn_dt_bias_b[l], dn_norm_w[l], w_out[l],
                   xa_wq[l], xa_wkv[l], xa_wo[l], mem_norm_g[l], mlp_w1[l], mlp_w2[l],
                   norm_pre_mix[l], norm_post_mix[l], norm_pre_xa[l], norm_post_xa[l],
                   norm_pre_mlp[l], norm_post_mlp[l])
    return x


def setup_inputs(seed: int = 0) -> dict:
    key = jax.random.key(seed)
    ks = jax.random.split(key, 32)
    nrm = lambda k, shape, fan_in: jax.random.normal(k, shape, F32) * (fan_in ** -0.5)
    gain = lambda k, n: 1.0 + 0.02 * jax.random.normal(k, (DEPTH, n), F32)
    dt = jnp.exp(jax.random.uniform(ks[8], (2, DEPTH, DN_HEADS), F32, np.log(1e-3), np.log(1e-1)))
    dt_bias = dt + jnp.log(-jnp.expm1(-dt))
    A_log = jnp.log(jax.random.uniform(ks[9], (2, DEPTH, DN_HEADS), F32, 1.0, 16.0))
    return {
        "x_prompt": jax.random.normal(ks[0], (BATCH, SEQ, D_MODEL), F32),
        "x_sample": jax.random.normal(ks[1], (DEC_BATCH, DEC_SEQ, D_MODEL), F32),
        "mem_prompt": jax.random.normal(ks[2], (BATCH, MEM_LEN, D_MODEL), F32),
        "mem_sample": jax.random.normal(ks[3], (DEC_BATCH, MEM_LEN, D_MODEL), F32),
        "w_in": nrm(ks[4], (DEPTH, D_MODEL, IN_W), D_MODEL),
        "attn_sink": 0.5 * jax.random.normal(ks[5], (DEPTH, ATT_HEADS), F32),
        "dn_conv_w": nrm(ks[6], (DEPTH, DN_CONV, 3 * DN_W), DN_CONV),
        "dn_A_log_f": A_log[0],
        "dn_A_log_b": A_log[1],
        "dn_dt_bias_f": dt_bias[0],
        "dn_dt_bias_b": dt_bias[1],
        "dn_norm_w": gain(ks[10], DN_HEAD_DIM),
        "w_out": nrm(ks[11], (DEPTH, MIX_W, D_MODEL), MIX_W),
        "xa_wq": nrm(ks[12], (DEPTH, D_MODEL, X_W), D_MODEL),
        "xa_wkv": nrm(ks[13], (DEPTH, D_MODEL, 2 * X_W), D_MODEL),
        "xa_wo": nrm(ks[14], (DEPTH, X_W, D_MODEL), X_W),
        "mem_norm_g": gain(ks[15], D_MODEL),
        "mlp_w1": nrm(ks[16], (DEPTH, D_MODEL, D_FF), D_MODEL),
        "mlp_w2": nrm(ks[17], (DEPTH, D_FF, D_MODEL), D_FF),
        "norm_pre_mix": gain(ks[18], D_MODEL),
        "norm_post_mix": gain(ks[19], D_MODEL),
        "norm_pre_xa": gain(ks[20], D_MODEL),
        "norm_post_xa": gain(ks[21], D_MODEL),
        "norm_pre_mlp": gain(ks[22], D_MODEL),
        "norm_post_mlp": gain(ks[23], D_MODEL),
    }


def reference(x_prompt, x_sample, mem_prompt, mem_sample, w_in, attn_sink, dn_conv_w,
              dn_A_log_f, dn_A_log_b, dn_dt_bias_f, dn_dt_bias_b, dn_norm_w, w_out,
              xa_wq, xa_wkv, xa_wo, mem_norm_g, mlp_w1, mlp_w2,
              norm_pre_mix, norm_post_mix, norm_pre_xa, norm_post_xa, norm_pre_mlp, norm_post_mlp):
    y_prompt = _trunk(x_prompt, mem_prompt, w_in, attn_sink, dn_conv_w, dn_A_log_f, dn_A_log_b,
                      dn_dt_bias_f, dn_dt_bias_b, dn_norm_w, w_out, xa_wq, xa_wkv, xa_wo,
                      mem_norm_g, mlp_w1, mlp_w2, norm_pre_mix, norm_post_mix, norm_pre_xa,
                      norm_post_xa, norm_pre_mlp, norm_post_mlp)
    y_sample = _trunk(x_sample, mem_sample, w_in, attn_sink, dn_conv_w, dn_A_log_f, dn_A_log_b,
                      dn_dt_bias_f, dn_dt_bias_b, dn_norm_w, w_out, xa_wq, xa_wkv, xa_wo,
                      mem_norm_g, mlp_w1, mlp_w2, norm_pre_mix, norm_post_mix, norm_pre_xa,
                      norm_post_xa, norm_pre_mlp, norm_post_mlp)
    return (y_prompt, y_sample)
```

```python
import numpy as np
import concourse.bass as bass
import concourse.mybir as mybir
from concourse.bass_utils import run_bass_kernel_spmd

F32 = mybir.dt.float32
BF16 = mybir.dt.bfloat16
ACT = mybir.ActivationFunctionType
ALU = mybir.AluOpType
AX = mybir.AxisListType

D = 1024
IN_W = 2832
EPS = 1e-6


class TK:
    def __init__(self, nc, sems):
        self.nc = nc
        self.sems = sems
        self.eng = {}
        for name, obj in (("pe", nc.tensor), ("act", nc.scalar), ("dve", nc.vector),
                          ("pool", nc.gpsimd), ("sp", nc.sync)):
            self.eng[name] = dict(obj=obj, sem=self.sems.pop(), cnt=0, known={})
        self.state = {}
        self.streams = {}
        self.strict = {"pe": False, "act": True, "dve": True, "pool": True, "sp": False}

    def _deps(self, r, w):
        deps = []
        for k in r:
            st = self.state.get(k)
            if st is not None and st["w"] is not None:
                deps.append(st["w"])
            if st is not None and isinstance(k, tuple) and k[0] in ("pf", "pb"):
                deps.extend(st["r"].values())
        for k in w:
            st = self.state.get(k)
            if st is not None:
                if st["w"] is not None:
                    deps.append(st["w"])
                deps.extend(st["r"].values())
        return deps

    def _wait(self, ename, deps):
        E = self.eng[ename]
        need = {}
        for (sem, val, src) in deps:
            if src == ename and not self.strict[ename]:
                continue
            key = id(sem)
            if E["known"].get(key, 0) >= val:
                continue
            if key not in need or need[key][1] < val:
                need[key] = (sem, val)
        for key, (sem, val) in need.items():
            E["obj"].wait_ge(sem, val)
            E["known"][key] = val

    def _book(self, r, w, tok, rkey):
        for k in r:
            st = self.state.get(k)
            if st is None:
                st = self.state[k] = {"w": None, "r": {}}
            st["r"][rkey] = tok
        for k in w:
            self.state[k] = {"w": tok, "r": {}}

    def op(self, ename, fn, r=(), w=()):
        self._wait(ename, self._deps(r, w))
        E = self.eng[ename]
        E["cnt"] += 1
        fn(E["obj"]).then_inc(E["sem"], 1)
        self._book(r, w, (E["sem"], E["cnt"], ename), ename)

    def dma(self, qname, out, in_, r=(), w=(), stream="d"):
        self._wait(qname, self._deps(r, w))
        if stream not in self.streams:
            self.streams[stream] = [self.sems.pop(), 0]
        S = self.streams[stream]
        S[1] += 1
        self.eng[qname]["obj"].dma_start(out=out, in_=in_).then_inc(S[0], 16)
        self._book(r, w, (S[0], 16 * S[1], "dma:" + stream), "dma:" + stream)

    def barrier(self):
        for ename, E in self.eng.items():
            for name, S in self.streams.items():
                if S[1] > 0 and E["known"].get(id(S[0]), 0) < 16 * S[1]:
                    E["obj"].wait_ge(S[0], 16 * S[1])
                    E["known"][id(S[0])] = 16 * S[1]
            for name, o in self.eng.items():
                if name != ename and o["cnt"] > 0 and E["known"].get(id(o["sem"]), 0) < o["cnt"]:
                    E["obj"].wait_ge(o["sem"], o["cnt"])
                    E["known"][id(o["sem"])] = o["cnt"]

    def finish(self, ename="pool"):
        E = self.eng[ename]
        for name, S in self.streams.items():
            if S[1] > 0:
                E["obj"].wait_ge(S[0], 16 * S[1])
        for name, o in self.eng.items():
            if name != ename and o["cnt"] > 0:
                E["obj"].wait_ge(o["sem"], o["cnt"])


def bc(ap, dims):
    return bass.AP(ap.tensor, ap.offset, [list(ap.ap[0])] + [list(d) for d in dims])


DEBUG = False
DBG_OUT = {}


class _Stop(Exception):
    pass


def build(NT, SEG, stop=None):
    nc_box = {}
    try:
        _build(NT, SEG, stop, nc_box)
    except _Stop:
        nc_box["tk"].finish("pool")
    return nc_box["nc"]


def _build(NT, SEG, stop, nc_box):
    T = NT * 128
    SEGT = SEG * 128
    PADW = SEGT + 4
    NMT = NT // 4
    nc = bass.Bass("TRN2", target_bir_lowering=False)
    nc_box["nc"] = nc

    def chk(name):
        if stop == name:
            raise _Stop()

    def din(name, shape, dt=F32):
        return nc.dram_tensor(name, shape, dt, kind="ExternalInput").ap()

    x_d = din("x", [T, D])
    mem_d = din("mem", [2, 256, D])
    w_in_d = din("w_in", [D, IN_W])
    w_out_d = din("w_out", [D, D])
    wq_d = din("xa_wq", [D, 512])
    wkv_d = din("xa_wkv", [D, 1024])
    wo_d = din("xa_wo", [512, D])
    w1_d = din("mlp_w1", [D, 4096])
    w2_d = din("mlp_w2", [4096, D])
    pp_d = din("pp", [128, 96])
    rows_d = din("rows", [128, 3232])
    cst_d = din("cst", [128, 2816])
    cs_d = din("cossin", [128, 2 * T])
    y_d = nc.dram_tensor("y", [T, D], F32, kind="ExternalOutput").ap()

    kw = {"kind": "ExternalOutput"} if DEBUG else {}
    dqkv_d = nc.dram_tensor("dqkv_s", [12, 128, T], BF16, **kw).ap()
    sz_d = nc.dram_tensor("sz_s", [T, 512], BF16, **kw).ap()
    mixT_d = nc.dram_tensor("mixT_s", [8, 128, T], BF16, **kw).ap()
    x2_d = nc.dram_tensor("x2_s", [T, D], F32, **kw).ap()

    from contextlib import ExitStack
    es = ExitStack()
    sems = [es.enter_context(nc.semaphore("s%d" % i)) for i in range(60)]
    tk = TK(nc, sems)
    nc_box["tk"] = tk

    def sb(stack, name, shape, dt):
        return stack.enter_context(nc.sbuf_tensor("sb_" + name, shape, dt))

    pf = [es.enter_context(nc.psum_tensor("pf%d" % i, [128, 512], F32)) for i in range(6)]
    pb = [es.enter_context(nc.psum_tensor("pb%d" % i, [128, 1024], BF16)) for i in range(2)]
    rr = {"f": 0, "b": 0}

    def nf():
        i = rr["f"]; rr["f"] = (i + 1) % 6
        return pf[i], ("pf", i)

    def nb():
        i = rr["b"]; rr["b"] = (i + 1) % 2
        return pb[i], ("pb", i)

    cst = sb(es, "cst", [128, 2816], F32)
    pp = sb(es, "pp", [128, 96], F32)
    rows = sb(es, "rows", [128, 160], F32)
    identb = sb(es, "identb", [128, 128], BF16)
    onesb = sb(es, "onesb", [128, 128], BF16)
    tk.dma("sp", cst[:], cst_d[:, :], w=["cst"], stream="c_cst")
    tk.dma("sp", pp[:], pp_d[:, :], w=["pp"], stream="c_pp")
    tk.dma("sp", rows[:], rows_d[:, 3072:3232], w=["rows"], stream="c_rows")
    C_ID, C_MIF, C_MSF, C_MIB, C_MSB, C_NSF, C_NSB, C_MTF, C_MTB, C_I0, C_I1, C_ONE, C_PROT = [i * 128 for i in range(13)]
    C_MPREV = 13 * 128
    C_MNEXT = C_MPREV + 512
    ident = cst[:, C_ID:C_ID + 128]
    tk.op("dve", lambda e: e.tensor_copy(out=identb[:], in_=ident), r=["cst"], w=["identb"])
    tk.op("dve", lambda e: e.tensor_copy(out=onesb[:], in_=cst[:, C_ONE:C_ONE + 128]), r=["cst"], w=["onesb"])
    P_GMIX, P_GXA, P_GMLP, P_GMEM, P_CONV, P_FLAG = 0, 8, 16, 24, 32, 92
    flag = pp[:, P_FLAG:P_FLAG + 1]
    R_DNW, R_SINK, R_ALOG, R_DT = 0, 128, 136, 144

    sm = sb(es, "sm", [128, 64], F32)
    smc = {"i": 0}

    def smcol(n=1):
        i = smc["i"]
        if i + n > 64:
            i = 0
        smc["i"] = i + n
        return sm[:, i:i + n], ("sm", i // 8)

    def rstd_from_ms(ms_ap, mskey, out_ap, outkey, n_eps=EPS):
        tk.op("act", lambda e: e.activation(out=out_ap, in_=ms_ap, func=ACT.Ln, bias=n_eps, scale=1.0),
              r=[mskey], w=[outkey])
        tk.op("act", lambda e: e.activation(out=out_ap, in_=out_ap, func=ACT.Exp, scale=-0.5),
              r=[outkey], w=[outkey])

    lw_cnt = [0]

    def load_weight_bf16(stack_stage, dst, dstkey, src_d, nk, ncols, gcol=None, post=None, engs=("act", "dve")):
        stg = stack_stage
        for kc in range(nk):
            sl, skey, sname = stg[lw_cnt[0] % len(stg)]
            lw_cnt[0] += 1
            tk.dma("sp", sl[:, 0:ncols], src_d[kc * 128:(kc + 1) * 128, :], w=[skey], stream=sname)
            en = engs[kc % len(engs)]
            if post is not None:
                post(kc, sl, skey, en)
                continue
            if gcol is None:
                if en == "act":
                    tk.op("act", lambda e, kc=kc, sl=sl: e.activation(out=dst[:, kc, :], in_=sl[:, 0:ncols], func=ACT.Copy),
                          r=[skey], w=[dstkey])
                else:
                    tk.op(en, lambda e, kc=kc, sl=sl: e.tensor_copy(out=dst[:, kc, :], in_=sl[:, 0:ncols]),
                          r=[skey], w=[dstkey])
            else:
                g = pp[:, gcol + kc:gcol + kc + 1]
                if en == "act":
                    tk.op("act", lambda e, kc=kc, sl=sl, g=g: e.activation(out=dst[:, kc, :], in_=sl[:, 0:ncols], func=ACT.Copy, scale=g),
                          r=[skey, "pp"], w=[dstkey])
                else:
                    tk.op(en, lambda e, kc=kc, sl=sl, g=g: e.tensor_scalar(out=dst[:, kc, :], in0=sl[:, 0:ncols], scalar1=g, scalar2=None, op0=ALU.mult),
                          r=[skey, "pp"], w=[dstkey])

    def norm_part(xs, xkey, hb, hbkey, junk, junkkey):
        ms, mk = smcol()
        tk.op("act", lambda e: e.activation(out=junk, in_=xs, func=ACT.Square, scale=1.0 / 32.0, accum_out=ms),
              r=[xkey], w=[junkkey, mk])
        rstd_from_ms(ms, mk, ms, mk)
        tk.op("dve", lambda e: e.tensor_scalar(out=hb, in0=xs, scalar1=ms, scalar2=None, op0=ALU.mult),
              r=[xkey, mk], w=[hbkey])

    def transpose_part(hb, hbkey, hT_out, hTkey):
        pbt, pk = nb()
        for kc in range(8):
            tk.op("pe", lambda e, kc=kc: e.transpose(pbt[:, kc * 128:(kc + 1) * 128], hb[:, kc * 128:(kc + 1) * 128], identb[:]),
                  r=[hbkey, "identb"], w=[pk])
        tk.op("act", lambda e: e.activation(out=hT_out, in_=pbt[:, :].rearrange("p (k t) -> p k t", k=8), func=ACT.Copy),
              r=[pk], w=[hTkey])

    def norm_transpose(xs, xkey, hb, hbkey, hT_out, hTkey, junk, junkkey):
        ms, mk = smcol()
        tk.op("act", lambda e: e.activation(out=junk, in_=xs, func=ACT.Square, scale=1.0 / 32.0, accum_out=ms),
              r=[xkey], w=[junkkey, mk])
        rstd_from_ms(ms, mk, ms, mk)
        tk.op("dve", lambda e: e.tensor_scalar(out=hb, in0=xs, scalar1=ms, scalar2=None, op0=ALU.mult),
              r=[xkey, mk], w=[hbkey])
        pbt, pk = nb()
        for kc in range(8):
            tk.op("pe", lambda e, kc=kc: e.transpose(pbt[:, kc * 128:(kc + 1) * 128], hb[:, kc * 128:(kc + 1) * 128], identb[:]),
                  r=[hbkey, "identb"], w=[pk])
        tk.op("act", lambda e: e.activation(out=hT_out, in_=pbt[:, :].rearrange("p (k t) -> p k t", k=8), func=ACT.Copy),
              r=[pk], w=[hTkey])

    def post_norm_residual(banks, bkeys, grow, growkey, res_ap, reskey, out_ap, outkey, junk, junkkey, tmp, tmpkey, add_eng="pool"):
        m0, k0 = smcol(2)
        for hf in range(2):
            tk.op("act", lambda e, hf=hf: e.activation(out=junk[:, 0:512], in_=banks[hf][:, 0:512], func=ACT.Square, scale=1.0 / 32.0,
                                                       accum_out=m0[:, hf:hf + 1]),
                  r=[bkeys[hf]], w=[junkkey, k0])
        tk.op("dve", lambda e: e.tensor_tensor(out=m0[:, 0:1], in0=m0[:, 0:1], in1=m0[:, 1:2], op=ALU.add), r=[k0], w=[k0])
        rstd_from_ms(m0[:, 0:1], k0, m0[:, 0:1], k0)
        for hf in range(2):
            tk.op("dve", lambda e, hf=hf: e.scalar_tensor_tensor(out=tmp[:, hf * 512:(hf + 1) * 512], in0=banks[hf][:, 0:512], scalar=m0[:, 0:1],
                                                                 in1=grow[:, hf * 512:(hf + 1) * 512],
                                                                 op0=ALU.mult, op1=ALU.mult),
                  r=[bkeys[hf], k0, growkey], w=[tmpkey])
        tk.op(add_eng, lambda e: e.tensor_tensor(out=out_ap, in0=tmp, in1=res_ap, op=ALU.add), r=[tmpkey, reskey], w=[outkey])

    pgs = ExitStack()
    beta = sb(pgs, "beta", [128, NT, 8], F32)
    gg = sb(pgs, "gg", [128, NT, 8], F32)
    gc = sb(pgs, "gc", [128, NT, 8], F32)
    glb = sb(pgs, "glb", [128, NT, 2, 8], F32)
    kds = sb(pgs, "kds", [128, NT, 8], F32)
    bg = sb(pgs, "bg", [128, NT, 8], F32)
    egc = sb(pgs, "egc", [128, NT, 8], F32)
    pq = ExitStack()
    qT = sb(pq, "qT", [128, NT, 4, 128], BF16)
    kT = sb(pq, "kT", [128, T], BF16)
    vaug = sb(pq, "vaug", [128, NT, 2, 65], BF16)
    gates = sb(pq, "gates", [128, NT, 16], F32)
    with ExitStack() as pa:
        stgA = [sb(pa, "stgA%d" % i, [128, 1416], F32) for i in range(2)]
        w_in_b = sb(pa, "w_in_b", [128, 8, IN_W], BF16)
        protb = sb(pa, "protb", [128, 128], BF16)
        csb = [sb(pa, "cs%d" % i, [128, 2, 512], F32) for i in range(2)]
        xb2 = [sb(pa, "xa%d" % i, [128, D], F32) for i in range(2)]
        hb4A = [sb(pa, "hbA%d" % i, [128, D], BF16) for i in range(4)]
        junk = sb(pa, "junkA", [128, D], BF16)
        hT = [sb(pa, "hTA%d" % i, [128, 8, 512], BF16) for i in range(2)]
        qraw = [sb(pa, "qraw%d" % i, [128, 512], BF16) for i in range(2)]
        t1 = [sb(pa, "t1_%d" % i, [128, 512], F32) for i in range(2)]
        t2 = [sb(pa, "t2_%d" % i, [128, 512], F32) for i in range(2)]
        spl = [sb(pa, "spl%d" % i, [128, 512], BF16) for i in range(3)]
        ez = [sb(pa, "ez%d" % i, [128, 512], F32) for i in range(2)]
        szb = [sb(pa, "szb%d" % i, [128, 512], BF16) for i in range(2)]

        tk.op("dve", lambda e: e.tensor_copy(out=protb[:], in_=cst[:, C_PROT:C_PROT + 128]), r=["cst"], w=["protb"])
        tk.op("pool", lambda e: e.memset(vaug[:], 1.0), w=["vaug"])

        for kc in range(8):
            g = pp[:, P_GMIX + kc:P_GMIX + kc + 1]
            for hv in range(2):
                sl, skey = stgA[hv], ("stgA", hv)
                tk.dma("sp", sl[:], w_in_d[kc * 128:(kc + 1) * 128, hv * 1416:(hv + 1) * 1416], w=[skey], stream="stgA%d" % hv)
                if hv == 0:
                    src_q = sl[:, 0:512].rearrange("p (a i d) -> p i a d", a=2, i=4)
                    dst_q = w_in_b[:, kc, 0:512].rearrange("p (i a d) -> p i a d", a=2, i=4)
                    tk.op("dve", lambda e, g=g, src_q=src_q, dst_q=dst_q: e.tensor_scalar(out=dst_q, in0=src_q, scalar1=g, scalar2=None, op0=ALU.mult),
                          r=[skey, "pp"], w=["w_in_b"])
                    tk.op("act", lambda e, g=g, sl=sl, kc=kc: e.activation(out=w_in_b[:, kc, 512:1416], in_=sl[:, 512:1416], func=ACT.Copy, scale=g),
                          r=[skey, "pp"], w=["w_in_b"])
                else:
                    tk.op("act", lambda e, g=g, sl=sl, kc=kc: e.activation(out=w_in_b[:, kc, 1416:2832], in_=sl[:, 0:1416], func=ACT.Copy, scale=g),
                          r=[skey, "pp"], w=["w_in_b"])

        chk("W")

        def a_norm(mt_):
            for s in range(4):
                t = mt_ * 4 + s
                xs = xb2[t % 2]
                xk = ("xa", t % 2)
                tk.dma("sp", xs[:], x_d[t * 128:(t + 1) * 128, :], w=[xk], stream="xa%d" % (t % 2))
                norm_part(xs[:], xk, hb4A[s][:], ("hbA", s), junk[:], "junkA")

        for mt in range(NMT):
            hTm = hT[mt % 2]
            hk = ("hTA", mt % 2)
            cs, csk = csb[mt % 2], ("cs", mt % 2)
            tk.dma("sp", cs[:, 0, :], cs_d[:, mt * 512:(mt + 1) * 512], w=[csk], stream="cs%d" % (mt % 2))
            tk.dma("sp", cs[:, 1, :], cs_d[:, T + mt * 512:T + (mt + 1) * 512], w=[csk], stream="cs%d" % (mt % 2))
            if mt == 0:
                a_norm(0)
            for s in range(4):
                transpose_part(hb4A[s][:], ("hbA", s), hTm[:, :, s * 128:(s + 1) * 128], hk)
            chk("A_nt")
            for ct in range(17):
                if ct == 5:
                    chk("A_q")
                c0 = ct * 128 if ct < 5 else 768 + (ct - 5) * 128
                bank, bk = nf()
                for kc in range(8):
                    tk.op("pe", lambda e, kc=kc, c0=c0: e.matmul(bank[:, 0:512], lhsT=w_in_b[:, kc, c0:c0 + 128], rhs=hTm[:, kc, :],
                                                                start=(kc == 0), stop=(kc == 7)),
                          r=["w_in_b", hk], w=[bk])
                chk("A_m0")
                if ct < 5:
                    i2 = ct % 2
                    qr, qk_ = qraw[i2], ("qraw", i2)
                    tk.op("act", lambda e, qr=qr: e.activation(out=qr[:], in_=bank[:, 0:512], func=ACT.Copy), r=[bk], w=[qk_])
                    bank2, bk2 = nf()
                    tk.op("pe", lambda e, qr=qr, bank2=bank2: e.matmul(bank2[:, 0:512], lhsT=protb[:], rhs=qr[:], start=True, stop=True),
                          r=["protb", qk_], w=[bk2])
                    chk("A_m1")
                    a1, a1k, a2, a2k = t1[i2], ("t1", i2), t2[i2], ("t2", i2)
                    tk.op("dve", lambda e, a1=a1: e.tensor_tensor(out=a1[:], in0=bank[:, 0:512], in1=cs[:, 0, :], op=ALU.mult),
                          r=[bk, csk], w=[a1k])
                    tk.op("dve", lambda e, a2=a2, bank2=bank2: e.tensor_tensor(out=a2[:], in0=bank2[:, 0:512], in1=cs[:, 1, :], op=ALU.mult),
                          r=[bk2, csk], w=[a2k])
                    chk("A_m2")
                    if ct < 4:
                        dst = qT[:, mt * 4:(mt + 1) * 4, ct, :]
                        tk.op("dve", lambda e, a1=a1, a2=a2, dst=dst: e.tensor_tensor(out=dst, in0=a1[:].rearrange("p (s t) -> p s t", s=4),
                                                                                       in1=a2[:].rearrange("p (s t) -> p s t", s=4), op=ALU.add),
                              r=[a1k, a2k], w=["qT"])
                    else:
                        tk.op("dve", lambda e, a1=a1, a2=a2: e.tensor_tensor(out=kT[:, mt * 512:(mt + 1) * 512], in0=a1[:], in1=a2[:], op=ALU.add),
                              r=[a1k, a2k], w=["kT"])
                else:
                    i3 = ct % 3
                    sp_, spk = spl[i3], ("spl", i3)
                    if ct % 2 == 0:
                        tk.op("act", lambda e, sp_=sp_: e.activation(out=sp_[:], in_=bank[:, 0:512], func=ACT.Copy), r=[bk], w=[spk])
                    else:
                        tk.op("dve", lambda e, sp_=sp_: e.tensor_copy(out=sp_[:], in_=bank[:, 0:512]), r=[bk], w=[spk])
                    tk.dma("pool", dqkv_d[ct - 5, :, mt * 512:(mt + 1) * 512], sp_[:], r=[spk], w=[("dqkv_d", ct - 5, mt)], stream="spl%d" % i3)
            chk("A_fm")
            if mt + 1 < NMT:
                a_norm(mt + 1)
            for s in range(4):
                t = mt * 4 + s
                bankA, bkA = nf()
                for kc in range(8):
                    tk.op("pe", lambda e, kc=kc: e.matmul(bankA[:, 0:128], lhsT=hTm[:, kc, s * 128:(s + 1) * 128], rhs=w_in_b[:, kc, 640:768],
                                                          start=(kc == 0), stop=(kc == 7)), r=["w_in_b", hk], w=[bkA])
                for kc in range(8):
                    tk.op("pe", lambda e, kc=kc: e.matmul(bankA[:, 128:144], lhsT=hTm[:, kc, s * 128:(s + 1) * 128], rhs=w_in_b[:, kc, 2816:2832],
                                                          start=(kc == 0), stop=(kc == 7)), r=["w_in_b", hk], w=[bkA])
                bankB, bkB = nf()
                for kc in range(8):
                    tk.op("pe", lambda e, kc=kc: e.matmul(bankB[:, 0:512], lhsT=hTm[:, kc, s * 128:(s + 1) * 128], rhs=w_in_b[:, kc, 2304:2816],
                                                          start=(kc == 0), stop=(kc == 7)), r=["w_in_b", hk], w=[bkB])
                tk.op("act", lambda e, t=t: e.activation(out=vaug[:, t, :, 0:64], in_=bankA[:, 0:128].rearrange("p (a d) -> p a d", a=2), func=ACT.Copy),
                      r=[bkA], w=["vaug"])
                tk.op("dve", lambda e, t=t: e.tensor_copy(out=gates[:, t, :], in_=bankA[:, 128:144]), r=[bkA], w=["gates"])
                e_, ek = ez[s % 2], ("ez", s % 2)
                z_, zk = szb[s % 2], ("szb", s % 2)
                tk.op("act", lambda e, e_=e_: e.activation(out=e_[:], in_=bankB[:, 0:512], func=ACT.Exp, scale=-1.0), r=[bkB], w=[ek])
                tk.op("act", lambda e, e_=e_: e.activation(out=e_[:], in_=e_[:], func=ACT.Ln, bias=1.0), r=[ek], w=[ek])
                tk.op("act", lambda e, e_=e_: e.activation(out=e_[:], in_=e_[:], func=ACT.Exp, scale=-1.0), r=[ek], w=[ek])
                tk.op("dve", lambda e, e_=e_, z_=z_: e.tensor_tensor(out=z_[:], in0=bankB[:, 0:512], in1=e_[:], op=ALU.mult), r=[bkB, ek], w=[zk])
                tk.dma("pool", sz_d[t * 128:(t + 1) * 128, :], z_[:], r=[zk], w=[("sz_d", t)], stream="szb%d" % (s % 2))

    chk("A")
    tk.barrier()
    if True:
        with ExitStack() as pa2:
            mprev = sb(pa2, "mprev", [128, 512], BF16)
            mnext = sb(pa2, "mnext", [128, 512], BF16)
            mprevf = sb(pa2, "mprevf", [128, 512], BF16)
            mnextf = sb(pa2, "mnextf", [128, 512], BF16)
            esink = sb(pa2, "esink", [128, 8], F32)
            pT = [sb(pa2, "pT%d" % i, [128, 512], BF16) for i in range(24)]
            attb = [sb(pa2, "attb%d" % i, [128, 512], BF16) for i in range(4)]
            attTs = [sb(pa2, "attTs%d" % i, [128, 512], BF16) for i in range(4)]
            den = sb(pa2, "den", [128, 4, 8], F32)
            tk.op("dve", lambda e: e.tensor_copy(out=mprev[:], in_=cst[:, C_MPREV:C_MPREV + 512]), r=["cst"], w=["mprev"])
            tk.op("dve", lambda e: e.tensor_copy(out=mnext[:], in_=cst[:, C_MNEXT:C_MNEXT + 512]), r=["cst"], w=["mnext"])
            tk.op("dve", lambda e: e.tensor_scalar(out=mprevf[:], in0=cst[:, C_MPREV:C_MPREV + 512], scalar1=flag, scalar2=None, op0=ALU.mult),
                  r=["cst", "pp"], w=["mprevf"])
            tk.op("dve", lambda e: e.tensor_scalar(out=mnextf[:], in0=cst[:, C_MNEXT:C_MNEXT + 512], scalar1=flag, scalar2=None, op0=ALU.mult),
                  r=["cst", "pp"], w=["mnextf"])
            tk.op("act", lambda e: e.activation(out=esink[:], in_=rows[:, R_SINK:R_SINK + 8], func=ACT.Exp), r=["rows"], w=["esink"])
            freeA = list(range(6))

            def getA(n):
                while len(freeA) < n:
                    yield
                out = []
                for _ in range(n):
                    i = freeA.pop(0)
                    out.append((pf[i], ("pf", i)))
                return out

            def relA(*bks):
                for bk in bks:
                    freeA.append(bk[1])

            NCH = 4

            def attn(j, cs_):
                ab, abk = attb[cs_], ("attb", cs_)
                for kv in range(2):
                    jjs = [jj for jj in (j - 1, j, j + 1) if 0 <= jj < NT]
                    bl = yield from getA(len(jjs))
                    pts = []
                    for n, jj in enumerate(jjs):
                        bank, bk = bl[n]
                        tk.op("pe", lambda e, jj=jj, bank=bank: e.matmul(bank[:, 0:512], lhsT=kT[kv * 64:(kv + 1) * 64, jj * 128:(jj + 1) * 128],
                                                                       rhs=qT[kv * 64:(kv + 1) * 64, j, :, :], start=True, stop=True),
                              r=["kT", "qT"], w=[bk])
                    yield
                    for n, jj in enumerate(jjs):
                        bank, bk = bl[n]
                        p_, pk_ = pT[cs_ * 6 + kv * 3 + n], ("pT", cs_ * 6 + kv * 3 + n)
                        tk.op("act", lambda e, p_=p_, bank=bank: e.activation(out=p_[:], in_=bank[:, 0:512], func=ACT.Exp, scale=0.125), r=[bk], w=[pk_])
                        pts.append((jj, p_, pk_))
                    relA(*[b[1] for b in bl])
                    yield
                    for (jj, p_, pk_) in pts:
                        if jj != j:
                            cross = (jj // SEG) != (j // SEG)
                            if jj < j:
                                m_, mk_ = (mprevf, "mprevf") if cross else (mprev, "mprev")
                            else:
                                m_, mk_ = (mnextf, "mnextf") if cross else (mnext, "mnext")
                            tk.op("pool" if jj < j else "dve", lambda e, p_=p_, m_=m_: e.tensor_tensor(out=p_[:], in0=p_[:], in1=m_[:], op=ALU.mult), r=[pk_, mk_], w=[pk_])
                    yield
                    ((bo, bok),) = yield from getA(1)
                    for i in range(4):
                        for n, (jj, p_, pk_) in enumerate(pts):
                            tk.op("pe", lambda e, i=i, jj=jj, p_=p_, n=n: e.matmul(bo[:, i * 65:(i + 1) * 65], lhsT=p_[:, i * 128:(i + 1) * 128],
                                                                                 rhs=vaug[:, jj, kv, :], start=(n == 0), stop=(n == len(pts) - 1)),
                                  r=[pk_, "vaug"], w=[bok])
                    yield
                    bo3 = bo[:, 0:260].rearrange("p (i c) -> p i c", i=4)
                    dn_ = den[:, cs_, kv * 4:(kv + 1) * 4]
                    dk_ = ("den", cs_)
                    tk.op("dve", lambda e: e.tensor_tensor(out=dn_, in0=bo3[:, :, 64], in1=esink[:, kv * 4:(kv + 1) * 4], op=ALU.add),
                          r=[bok, "esink"], w=[dk_])
                    tk.op("dve", lambda e: e.reciprocal(out=dn_, in_=dn_), r=[dk_], w=[dk_])
                    tk.op("dve", lambda e: e.tensor_tensor(out=ab[:, kv * 256:(kv + 1) * 256].rearrange("p (i d) -> p i d", i=4),
                                                           in0=bo3[:, :, 0:64], in1=bc(dn_, [[1, 4], [0, 64]]), op=ALU.mult),
                          r=[bok, dk_], w=[abk])
                    relA(bok)
                    yield
                pbt, pk = nb()
                for kc in range(4):
                    tk.op("pe", lambda e, kc=kc: e.transpose(pbt[:, kc * 128:(kc + 1) * 128], ab[:, kc * 128:(kc + 1) * 128], identb[:]),
                          r=[abk, "identb"], w=[pk])
                at_, atk = attTs[cs_], ("attTs", cs_)
                tk.op("act", lambda e: e.activation(out=at_[:], in_=pbt[:, 0:512], func=ACT.Copy), r=[pk], w=[atk])
                tk.dma("pool", mixT_d[0:4, :, j * 128:(j + 1) * 128].rearrange("k p t -> p k t"), at_[:].rearrange("p (k t) -> p k t", k=4),
                       r=[atk], w=[("mixT_d", "a", j)], stream="attT%d" % cs_)
                yield

            nxt_j = 0
            active = []
            while nxt_j < NT or active:
                while nxt_j < NT and len(active) < NCH and all(a[0] % NCH != nxt_j % NCH for a in active):
                    active.append([nxt_j, attn(nxt_j, nxt_j % NCH)])
                    nxt_j += 1
                still = []
                for a in active:
                    try:
                        next(a[1])
                        still.append(a)
                    except StopIteration:
                        pass
                active = still

        chk("A2")
        tk.barrier()
        NG = NT * 8
        with ExitStack() as pg:
            negA = sb(pg, "negA", [128, 8], F32)
            tmpg = sb(pg, "tmpg", [128, NT, 8], F32)
            tk.op("act", lambda e: e.activation(out=beta[:], in_=gates[:, :, 0:8], func=ACT.Exp, scale=-1.0), r=["gates"], w=["beta"])
            tk.op("dve", lambda e: e.tensor_scalar(out=beta[:], in0=beta[:], scalar1=1.0, scalar2=None, op0=ALU.add), r=["beta"], w=["beta"])
            tk.op("dve", lambda e: e.reciprocal(out=beta[:], in_=beta[:]), r=["beta"], w=["beta"])
            tk.op("act", lambda e: e.activation(out=negA[:], in_=rows[:, R_ALOG:R_ALOG + 8], func=ACT.Exp), r=["rows"], w=["negA"])
            tk.op("dve", lambda e: e.tensor_scalar(out=negA[:], in0=negA[:], scalar1=-1.0, scalar2=None, op0=ALU.mult), r=["negA"], w=["negA"])
            tk.op("dve", lambda e: e.tensor_tensor(out=tmpg[:], in0=gates[:, :, 8:16], in1=bc(rows[:, R_DT:R_DT + 8], [[0, NT], [1, 8]]), op=ALU.add),
                  r=["gates", "rows"], w=["tmpg"])
            tk.op("act", lambda e: e.activation(out=tmpg[:], in_=tmpg[:], func=ACT.Exp), r=["tmpg"], w=["tmpg"])
            tk.op("act", lambda e: e.activation(out=tmpg[:], in_=tmpg[:], func=ACT.Ln, bias=1.0), r=["tmpg"], w=["tmpg"])
            tk.op("dve", lambda e: e.tensor_tensor(out=gg[:], in0=tmpg[:], in1=bc(negA[:, 0:8], [[0, NT], [1, 8]]), op=ALU.mult),
                  r=["tmpg", "negA"], w=["gg"])
            ggf = gg[:, :, :].rearrange("p t h -> p (t h)")
            for c0 in range(0, NG, 512):
                c1 = min(NG, c0 + 512)
                tsl = slice(c0 // 8, c1 // 8)
                for d_, cm in ((0, C_MIF), (1, C_MIB)):
                    bank, bk = nf()
                    tk.op("pe", lambda e, bank=bank, cm=cm: e.matmul(bank[:, 0:c1 - c0], lhsT=cst[:, cm:cm + 128], rhs=ggf[:, c0:c1], start=True, stop=True),
                          r=["cst", "gg"], w=[bk])
                    tk.op("dve", lambda e, bank=bank, d_=d_: e.tensor_copy(out=gc[:, tsl, d_ * 4:(d_ + 1) * 4],
                                                                          in_=bank[:, 0:c1 - c0].rearrange("p (t h) -> p t h", h=8)[:, :, d_ * 4:(d_ + 1) * 4]),
                          r=[bk], w=["gc"])
                for jx, cm in ((0, C_I0), (1, C_I1)):
                    bank, bk = nf()
                    tk.op("pe", lambda e, bank=bank, cm=cm: e.matmul(bank[:, 0:c1 - c0], lhsT=cst[:, cm:cm + 128], rhs=ggf[:, c0:c1], start=True, stop=True),
                          r=["cst", "gg"], w=[bk])
                    ps_ = slice(jx * 64, (jx + 1) * 64)
                    tk.op("dve", lambda e, bank=bank, ps_=ps_: e.tensor_tensor(out=kds[ps_, tsl, :], in0=bank[ps_, 0:c1 - c0].rearrange("p (t h) -> p t h", h=8),
                                                                              in1=gc[ps_, tsl, :], op=ALU.subtract),
                          r=[bk, "gc"], w=["kds"])
                    tk.op("act", lambda e, bank=bank, jx=jx: e.activation(out=glb[:, tsl, jx, :], in_=bank[:, 0:c1 - c0].rearrange("p (t h) -> p t h", h=8), func=ACT.Exp),
                          r=[bk], w=["glb"])
            tk.op("act", lambda e: e.activation(out=kds[:], in_=kds[:], func=ACT.Exp), r=["kds"], w=["kds"])
            tk.op("act", lambda e: e.activation(out=egc[:], in_=gc[:], func=ACT.Exp), r=["gc"], w=["egc"])
            tk.op("dve", lambda e: e.tensor_tensor(out=bg[:], in0=egc[:], in1=beta[:], op=ALU.mult), r=["egc", "beta"], w=["bg"])

    chk("G")
    tk.barrier()
    pq.close()
    with ExitStack() as pbk:
        diag = sb(pbk, "diag", [128, 15, 128], BF16)
        pad = [sb(pbk, "pad%d" % i, [128, 2 * PADW], BF16) for i in range(3)]
        kq = sb(pbk, "kq", [128, NT, 256], BF16)
        dnTb = sb(pbk, "dnTb", [128, T], BF16)
        ktok = sb(pbk, "ktok", [128, NT, 128], BF16)
        vtok = sb(pbk, "vtok", [128, NT, 128], BF16)
        ysil = [sb(pbk, "ysil%d" % i, [128, 512], F32) for i in range(4)]
        eb = [sb(pbk, "eb%d" % i, [128, 512], F32) for i in range(4)]
        sqb = [sb(pbk, "sqb%d" % i, [128, 512], BF16) for i in range(4)]
        o_dir = [sb(pbk, "o_dir%d" % i, [128, NT, 128], F32) for i in range(2)]
        Sf = [sb(pbk, "Sf%d" % i, [128, 128], F32) for i in range(2)]
        Sb = [[sb(pbk, "Sb%d_%d" % (i, j), [128, 128], BF16) for j in range(2)] for i in range(2)]
        NS = 4
        NPRE = 6
        Mg = [[sb(pbk, "Mg%d_%d" % (d_, s_), [128, 128], F32) for s_ in range(NS)] for d_ in range(2)]
        DDE = [[sb(pbk, "DDE%d_%d" % (d_, s_), [128, 384], F32) for s_ in range(NS)] for d_ in range(2)]
        XY = [[[sb(pbk, "XY%d_%d_%d" % (d_, s_, v_), [128, 384], BF16) for v_ in range(2)] for s_ in range(NS)] for d_ in range(2)]
        TTb = [[sb(pbk, "TT%d_%d" % (d_, s_), [128, 128], BF16) for s_ in range(NS)] for d_ in range(2)]
        UW = [[sb(pbk, "UW%d_%d" % (d_, s_), [128, 256], BF16) for s_ in range(NS)] for d_ in range(2)]
        aT = [[sb(pbk, "aT%d_%d" % (d_, s_), [128, 128], BF16) for s_ in range(NS)] for d_ in range(2)]
        qgT = [[sb(pbk, "qgT%d_%d" % (d_, s_), [128, 128], BF16) for s_ in range(NS)] for d_ in range(2)]
        kdb = [[sb(pbk, "kd%d_%d" % (d_, s_), [128, 128], BF16) for s_ in range(NS)] for d_ in range(2)]
        kbg = [[sb(pbk, "kbg%d_%d" % (d_, s_), [128, 128], BF16) for s_ in range(NS)] for d_ in range(2)]
        vbb = [[sb(pbk, "vb%d_%d" % (d_, s_), [128, 128], BF16) for s_ in range(NS)] for d_ in range(2)]
        vnew = [sb(pbk, "vnew%d" % d_, [128, 128], BF16) for d_ in range(2)]
        msn = sb(pbk, "msn", [128, NT], F32)
        szh, dnb, dnT = vtok, ktok, dnTb

        for h in range(4):
            for j in range(15):
                ct = (j // 5) * 4 + h
                tk.op("dve", lambda e, j=j, ct=ct: e.tensor_scalar(out=diag[:, j, :], in0=ident, scalar1=pp[:, P_CONV + ct * 5 + (j % 5):P_CONV + ct * 5 + (j % 5) + 1],
                                                                  scalar2=None, op0=ALU.mult), r=["cst", "pp"], w=["diag"])
            for w3 in range(3):
                ct = w3 * 4 + h
                pd, pdk = pad[w3], ("pad", w3)
                tk.op("pool", lambda e, pd=pd: e.memset(pd[:], 0.0), w=[pdk])
                for sg in range(2):
                    tk.dma("sp", pd[:, sg * PADW + 2:sg * PADW + 2 + SEGT], dqkv_d[ct, :, sg * SEGT:(sg + 1) * SEGT], r=[("dqkv_d", ct, m_) for m_ in range(NMT)], w=[pdk], stream="pad%d" % w3)
                tk.op("dve", lambda e, pd=pd: e.tensor_scalar(out=pd[:, SEGT + 2:SEGT + 4], in0=pd[:, PADW + 2:PADW + 4], scalar1=flag, scalar2=None, op0=ALU.mult),
                      r=[pdk, "pp"], w=[pdk])
                tk.op("dve", lambda e, pd=pd: e.tensor_scalar(out=pd[:, PADW:PADW + 2], in0=pd[:, SEGT:SEGT + 2], scalar1=flag, scalar2=None, op0=ALU.mult),
                      r=[pdk, "pp"], w=[pdk])
            free_banks = list(range(6))

            def get(n):
                while len(free_banks) < n:
                    yield
                out = []
                for _ in range(n):
                    i = free_banks.pop(0)
                    out.append((pf[i], ("pf", i)))
                return out

            def rel(*bks):
                for bk in bks:
                    free_banks.append(bk[1])

            def convblk(sg, b, w3, i2):
                tok0 = sg * SEGT + b * 512
                pd, pdk = pad[w3], ("pad", w3)
                e_, ek = eb[i2], ("eb", i2)
                y_, yk = ysil[i2], ("ysil", i2)
                s_, sk = sqb[i2], ("sqb", i2)
                ((bank, bk),) = yield from get(1)
                for j in range(5):
                    off = sg * PADW + b * 512 + j
                    tk.op("pe", lambda e, j=j, off=off: e.matmul(bank[:, 0:512], lhsT=diag[:, w3 * 5 + j, :], rhs=pd[:, off:off + 512],
                                                                  start=(j == 0), stop=(j == 4)), r=["diag", pdk], w=[bk])
                yield
                tk.op("act", lambda e: e.activation(out=e_[:], in_=bank[:, 0:512], func=ACT.Exp, scale=-1.0), r=[bk], w=[ek])
                yield
                tk.op("act", lambda e: e.activation(out=e_[:], in_=e_[:], func=ACT.Ln, bias=1.0), r=[ek], w=[ek])
                tk.op("act", lambda e: e.activation(out=e_[:], in_=e_[:], func=ACT.Exp, scale=-1.0), r=[ek], w=[ek])
                yield
                tk.op("dve", lambda e: e.tensor_tensor(out=y_[:], in0=bank[:, 0:512], in1=e_[:], op=ALU.mult), r=[bk, ek], w=[yk])
                rel(bk)
                yield
                if w3 < 2:
                    tk.op("act", lambda e: e.activation(out=s_[:], in_=y_[:], func=ACT.Square), r=[yk], w=[sk])
                    yield
                    ((bank2, bk2),) = yield from get(1)
                    tk.op("pe", lambda e: e.matmul(bank2[:, 0:512], lhsT=onesb[:], rhs=s_[:], start=True, stop=True), r=["onesb", sk], w=[bk2])
                    yield
                    tk.op("act", lambda e: e.activation(out=e_[:], in_=bank2[:, 0:512], func=ACT.Ln, bias=EPS), r=[bk2], w=[ek])
                    rel(bk2)
                    tk.op("act", lambda e: e.activation(out=e_[:], in_=e_[:], func=ACT.Exp, scale=-0.5), r=[ek], w=[ek])
                    yield
                    if w3 == 0:
                        tk.op("dve", lambda e: e.scalar_tensor_tensor(out=kq[:, tok0 // 128:tok0 // 128 + 4, 128:256], in0=y_[:].rearrange("p (a t) -> p a t", a=4),
                                                                      scalar=128.0 ** -0.5, in1=e_[:].rearrange("p (a t) -> p a t", a=4),
                                                                      op0=ALU.mult, op1=ALU.mult), r=[yk, ek], w=["qTh"])
                        yield
                    else:
                        tk.op("dve", lambda e: e.tensor_tensor(out=kq[:, tok0 // 128:tok0 // 128 + 4, 0:128], in0=y_[:].rearrange("p (a t) -> p a t", a=4),
                                                               in1=e_[:].rearrange("p (a t) -> p a t", a=4), op=ALU.mult), r=[yk, ek], w=["kTh"])
                        yield
                        pbt, pk = nb()
                        for q4 in range(4):
                            tk.op("pe", lambda e, q4=q4: e.transpose(pbt[:, q4 * 128:(q4 + 1) * 128], kq[:, tok0 // 128 + q4, 0:128], identb[:]),
                                  r=["kTh", "identb"], w=[pk])
                        tk.op("act", lambda e: e.activation(out=ktok[:, tok0 // 128:tok0 // 128 + 4, :], in_=pbt[:, 0:512].rearrange("p (a d) -> p a d", a=4), func=ACT.Copy),
                              r=[pk], w=["ktok"])
                        yield
                else:
                    tk.op("dve", lambda e: e.tensor_copy(out=s_[:], in_=y_[:]), r=[yk], w=[sk])
                    yield
                    pbt, pk = nb()
                    for q4 in range(4):
                        tk.op("pe", lambda e, q4=q4: e.transpose(pbt[:, q4 * 128:(q4 + 1) * 128], s_[:, q4 * 128:(q4 + 1) * 128], identb[:]),
                              r=[sk, "identb"], w=[pk])
                    tk.op("act", lambda e: e.activation(out=vtok[:, tok0 // 128:tok0 // 128 + 4, :], in_=pbt[:, 0:512].rearrange("p (a d) -> p a d", a=4), func=ACT.Copy),
                          r=[pk], w=["vtok"])
                    yield

            blocks = [(sg, b, w3) for sg in range(2) for b in range(SEGT // 512) for w3 in range(3)]
            NBUF = 4
            nxt_blk = 0
            active = []
            while nxt_blk < len(blocks) or active:
                while nxt_blk < len(blocks) and len(active) < NBUF and all(a[0] % NBUF != nxt_blk % NBUF for a in active):
                    sg, b, w3 = blocks[nxt_blk]
                    active.append([nxt_blk, convblk(sg, b, w3, nxt_blk % NBUF)])
                    nxt_blk += 1
                still = []
                for a in active:
                    try:
                        next(a[1])
                        still.append(a)
                    except StopIteration:
                        pass
                active = still
            chk("B0")
            CM = [(C_MIF, C_MSF, C_NSF, C_MTF), (C_MIB, C_MSB, C_NSB, C_MTB)]

            def pre(d_, t, s_):
                cmi, cms, cns, cmt = CM[d_]
                hd = d_ * 4 + h
                K = lambda nm: (nm, d_, s_)
                gcol = gg[:, t, hd:hd + 1]
                mg = Mg[d_][s_]
                tk.op("dve", lambda e: e.tensor_scalar(out=mg[:], in0=cst[:, cmi:cmi + 128], scalar1=gcol, scalar2=None, op0=ALU.mult),
                      r=["cst", "gg"], w=[K("Mg")])
                tk.op("dve", lambda e: e.tensor_scalar(out=kdb[d_][s_][:], in0=ktok[:, t, :], scalar1=kds[:, t, hd:hd + 1], scalar2=None, op0=ALU.mult),
                      r=["ktok", "kds"], w=[K("kd")])
                tk.op("dve", lambda e: e.tensor_scalar(out=kbg[d_][s_][:], in0=ktok[:, t, :], scalar1=bg[:, t, hd:hd + 1], scalar2=None, op0=ALU.mult),
                      r=["ktok", "bg"], w=[K("kbg")])
                tk.op("dve", lambda e: e.tensor_scalar(out=vbb[d_][s_][:], in0=vtok[:, t, :], scalar1=beta[:, t, hd:hd + 1], scalar2=None, op0=ALU.mult),
                      r=["vtok", "beta"], w=[K("vb")])
                yield
                ksl = kq[:, t, 0:128]
                qsl = kq[:, t, 128:256]
                dde = DDE[d_][s_]
                Dm_, DTm_, Eg_ = dde[:, 0:128], dde[:, 128:256], dde[:, 256:384]
                ((bank3, bk3),) = yield from get(1)
                for n_, (lh, rh) in enumerate(((mg[:], cst[:, cms:cms + 128]), (cst[:, cms:cms + 128], mg[:]), (cst[:, C_ONE:C_ONE + 128], mg[:]))):
                    tk.op("pe", lambda e, n_=n_, lh=lh, rh=rh: e.matmul(bank3[:, n_ * 128:(n_ + 1) * 128], lhsT=lh, rhs=rh, start=True, stop=True), r=["cst", K("Mg")], w=[bk3])
                yield
                tk.op("act", lambda e: e.activation(out=dde[:, 0:384], in_=bank3[:, 0:384], func=ACT.Exp), r=[bk3], w=[K("Dm"), K("DTm"), K("Eg")])
                rel(bk3)
                yield
                tk.op("pool", lambda e: e.tensor_tensor(out=qgT[d_][s_][:], in0=qsl, in1=Eg_, op=ALU.mult), r=["qTh", K("Eg")], w=[K("qgT")])
                ((bankG, bkG),) = yield from get(1)
                tk.op("pe", lambda e: e.matmul(bankG[:, 0:256], lhsT=ksl, rhs=kq[:, t, :], start=True, stop=True), r=["kTh", "qTh"], w=[bkG])
                yield
                xy0 = XY[d_][s_][0]
                X0 = xy0[:, 256:384]
                tk.op("dve", lambda e: e.tensor_tensor(out=dde[:, 0:256], in0=bankG[:, 0:256], in1=dde[:, 0:256], op=ALU.mult), r=[bkG, K("Dm"), K("DTm")], w=[K("Dm"), K("DTm")])
                rel(bkG)
                tk.op("dve", lambda e: e.scalar_tensor_tensor(out=X0, in0=Dm_, scalar=beta[:, t, hd:hd + 1], in1=cst[:, cns:cns + 128], op0=ALU.mult, op1=ALU.mult),
                      r=[K("Dm"), "beta", "cst"], w=[K("XY0")])
                tk.op("pool", lambda e: e.tensor_tensor(out=aT[d_][s_][:], in0=DTm_, in1=cst[:, cmt:cmt + 128], op=ALU.mult), r=[K("DTm"), "cst"], w=[K("aT")])
                yield
                bankT, bkT = nb()
                tk.op("pe", lambda e: e.transpose(bankT[:, 0:128], X0, identb[:]), r=[K("XY0"), "identb"], w=[bkT])
                tk.op("act", lambda e: e.activation(out=xy0[:, 0:128], in_=bankT[:, 0:128], func=ACT.Copy), r=[bkT], w=[K("XY0")])
                tk.op("dve", lambda e: e.tensor_tensor(out=xy0[:, 128:256], in0=bankT[:, 0:128], in1=ident, op=ALU.add), r=[bkT, "cst"], w=[K("XY0")])
                yield
                cur = 0
                for j in range(6):
                    xc, xck = XY[d_][s_][cur], K("XY%d" % cur)
                    nxt = 1 - cur
                    xn, xnk = XY[d_][s_][nxt], K("XY%d" % nxt)
                    Xc, Yc, Rc = xc[:, 256:384], xc[:, 0:128], xc[:, 128:256]
                    ((bank, bk),) = yield from get(1)
                    if j == 5:
                        tk.op("pe", lambda e: e.matmul(bank[:, 0:128], lhsT=Xc, rhs=Rc, start=True, stop=True), r=[xck], w=[bk])
                        yield
                        tk.op("dve", lambda e: e.tensor_tensor(out=TTb[d_][s_][:], in0=bank[:, 0:128], in1=Rc, op=ALU.add), r=[bk, xck], w=[K("TT")])
                        rel(bk)
                        yield
                        break
                    lo = 0 if j < 4 else 128
                    hi = 128 if j == 0 else 256
                    tk.op("pe", lambda e: e.matmul(bank[:, lo:hi], lhsT=Xc, rhs=xc[:, lo:hi], start=True, stop=True), r=[xck], w=[bk])
                    tk.op("pe", lambda e: e.matmul(bank[:, 256:384], lhsT=Yc, rhs=Xc, start=True, stop=True), r=[xck], w=[bk])
                    yield
                    if j < 4:
                        tk.op("act", lambda e: e.activation(out=bc(xn[:, 0:384], [[256, 2], [1, 128]]), in_=bc(bank[:, 0:384], [[256, 2], [1, 128]]), func=ACT.Copy),
                              r=[bk], w=[xnk])
                    else:
                        tk.op("act", lambda e: e.activation(out=xn[:, 256:384], in_=bank[:, 256:384], func=ACT.Copy), r=[bk], w=[xnk])
                    if j == 0:
                        tk.op("pool", lambda e: e.tensor_copy(out=xn[:, 128:256], in_=Rc), r=[xck], w=[xnk])
                    else:
                        tk.op("dve", lambda e: e.tensor_tensor(out=xn[:, 128:256], in0=bank[:, 128:256], in1=Rc, op=ALU.add), r=[bk, xck], w=[xnk])
                    rel(bk)
                    yield
                    cur = nxt
                ((bankU, bkU),) = yield from get(1)
                tk.op("pe", lambda e: e.matmul(bankU[:, 0:128], lhsT=TTb[d_][s_][:], rhs=vbb[d_][s_][:], start=True, stop=True), r=[K("TT"), K("vb")], w=[bkU])
                tk.op("pe", lambda e: e.matmul(bankU[:, 128:256], lhsT=kbg[d_][s_][:], rhs=TTb[d_][s_][:], start=True, stop=True), r=[K("TT"), K("kbg")], w=[bkU])
                yield
                tk.op("act", lambda e: e.activation(out=UW[d_][s_][:], in_=bankU[:, 0:256], func=ACT.Copy), r=[bkU], w=[K("u"), K("wTn")])
                rel(bkU)
                yield

            sbi = [0, 0]

            def scan(d_, t, s_):
                hd = d_ * 4 + h
                K = lambda nm: (nm, d_, s_)
                if (d_ == 0 and t == SEG) or (d_ == 1 and t == SEG - 1):
                    tk.op("dve", lambda e: e.tensor_scalar(out=Sf[d_][:], in0=Sf[d_][:], scalar1=flag, scalar2=None, op0=ALU.mult), r=[("Sf", d_), "pp"], w=[("Sf", d_)])
                    cb_ = sbi[d_]
                    tk.op("dve", lambda e: e.tensor_scalar(out=Sb[d_][cb_][:], in0=Sb[d_][cb_][:], scalar1=flag, scalar2=None, op0=ALU.mult),
                          r=[("Sb", d_, cb_), "pp"], w=[("Sb", d_, cb_)])
                    yield
                for ci in range(2):
                    ch = ci if d_ == 0 else 1 - ci
                    ps_ = slice(ch * 64, (ch + 1) * 64)
                    Sbc, Sbk = Sb[d_][sbi[d_]], ("Sb", d_, sbi[d_])
                    (bw, bwk), (bo_, bok_), (bs, bsk) = yield from get(3)
                    tk.op("pe", lambda e, bw=bw, Sbc=Sbc: e.matmul(bw[:, 0:128], lhsT=UW[d_][s_][:, 128:256], rhs=Sbc[:], start=True, stop=True), r=[K("wTn"), Sbk], w=[bwk])
                    tk.op("pe", lambda e, bo_=bo_, Sbc=Sbc: e.matmul(bo_[:, 0:128], lhsT=qgT[d_][s_][:], rhs=Sbc[:], start=True, stop=False), r=[K("qgT"), Sbk], w=[bok_])
                    yield
                    vn, vnk = vnew[d_], ("vnew", d_)
                    tk.op("dve", lambda e, bw=bw, vn=vn, ps_=ps_: e.tensor_tensor(out=vn[ps_, :], in0=UW[d_][s_][ps_, 0:128], in1=bw[ps_, 0:128], op=ALU.subtract), r=[bwk, K("u")], w=[vnk])
                    rel(bwk)
                    yield
                    tk.op("pe", lambda e, bo_=bo_, vn=vn, ps_=ps_: e.matmul(bo_[:, 0:128], lhsT=aT[d_][s_][ps_, :], rhs=vn[ps_, :], start=False, stop=True), r=[K("aT"), vnk], w=[bok_])
                    tk.op("pe", lambda e, bs=bs, vn=vn, ps_=ps_: e.matmul(bs[:, 0:128], lhsT=kdb[d_][s_][ps_, :], rhs=vn[ps_, :], start=True, stop=True), r=[K("kd"), vnk], w=[bsk])
                    yield
                    nsb = 1 - sbi[d_]
                    Sbn, Sbnk = Sb[d_][nsb], ("Sb", d_, nsb)
                    glc = glb[:, t, ch, hd:hd + 1]
                    tk.op("dve", lambda e, bs=bs, Sbn=Sbn, glc=glc: e.scalar_tensor_tensor(out=Sbn[:], in0=Sf[d_][:], scalar=glc, in1=bs[:, 0:128], op0=ALU.mult, op1=ALU.add),
                          r=[("Sf", d_), "glb", bsk], w=[Sbnk])
                    tk.op("dve", lambda e, bs=bs, glc=glc: e.scalar_tensor_tensor(out=Sf[d_][:], in0=Sf[d_][:], scalar=glc, in1=bs[:, 0:128], op0=ALU.mult, op1=ALU.add),
                          r=[("Sf", d_), "glb", bsk], w=[("Sf", d_)])
                    sbi[d_] = nsb
                    tk.op("act", lambda e, bo_=bo_, ps_=ps_: e.activation(out=o_dir[d_][ps_, t, :], in_=bo_[ps_, 0:128], func=ACT.Copy), r=[bok_], w=[("o_dir", d_)])
                    rel(bok_, bsk)
                    yield

            for d_ in range(2):
                tk.op("pool", lambda e, d_=d_: e.memset(Sf[d_][:], 0.0), w=[("Sf", d_)])
                tk.op("pool", lambda e, d_=d_: e.memset(Sb[d_][0][:], 0.0), w=[("Sb", d_, 0)])
                sbi[d_] = 0
            tile_of = lambda d_, i: i if d_ == 0 else NT - 1 - i
            pre_next = [0, 0]
            pre_done = [0, 0]
            scan_next = [0, 0]
            scan_done = [0, 0]
            active = []
            while scan_done[0] < NT or scan_done[1] < NT:
                for d_ in range(2):
                    while pre_next[d_] < NT and pre_next[d_] - scan_done[d_] < NS and sum(1 for a in active if a[0] == "pre") < NPRE:
                        i = pre_next[d_]
                        active.append(["pre", d_, i, pre(d_, tile_of(d_, i), i % NS)])
                        pre_next[d_] += 1
                    if scan_next[d_] < NT and scan_next[d_] == scan_done[d_] and pre_done[d_] > scan_next[d_]:
                        i = scan_next[d_]
                        active.append(["scan", d_, i, scan(d_, tile_of(d_, i), i % NS)])
                        scan_next[d_] += 1
                still = []
                for a in active:
                    try:
                        next(a[3])
                        still.append(a)
                    except StopIteration:
                        if a[0] == "pre":
                            a[0] = "pre_fin"
                            still.append(a)
                        else:
                            scan_done[a[1]] += 1
                for d_ in range(2):
                    while True:
                        fin = [a for a in still if a[0] == "pre_fin" and a[1] == d_ and a[2] == pre_done[d_]]
                        if not fin:
                            break
                        still.remove(fin[0])
                        pre_done[d_] += 1
                active = still

            if DEBUG and h == 3:
                dq = nc.dram_tensor("dbg_q", [128, T], BF16, kind="ExternalOutput").ap()
                dk = nc.dram_tensor("dbg_k", [128, T], BF16, kind="ExternalOutput").ap()
                do0 = nc.dram_tensor("dbg_o0", [128, NT * 128], F32, kind="ExternalOutput").ap()
                do1 = nc.dram_tensor("dbg_o1", [128, NT * 128], F32, kind="ExternalOutput").ap()
                dgg = nc.dram_tensor("dbg_gg", [128, NT * 8], F32, kind="ExternalOutput").ap()
                dbeta = nc.dram_tensor("dbg_beta", [128, NT * 8], F32, kind="ExternalOutput").ap()
                dkds = nc.dram_tensor("dbg_kds", [128, NT * 8], F32, kind="ExternalOutput").ap()
                dvt = nc.dram_tensor("dbg_vtok", [128, NT * 128], BF16, kind="ExternalOutput").ap()
                tk.dma("pool", do0[:, :], o_dir[0][:].rearrange("p t d -> p (t d)"), r=[("o_dir", 0)], stream="dbg")
                tk.dma("pool", do1[:, :], o_dir[1][:].rearrange("p t d -> p (t d)"), r=[("o_dir", 1)], stream="dbg")
                tk.dma("pool", dgg[:, :], gg[:].rearrange("p t d -> p (t d)"), r=["gg"], stream="dbg")
                tk.dma("pool", dbeta[:, :], beta[:].rearrange("p t d -> p (t d)"), r=["beta"], stream="dbg")
                tk.dma("pool", dkds[:, :], kds[:].rearrange("p t d -> p (t d)"), r=["kds"], stream="dbg")
                tk.dma("pool", dvt[:, :], vtok[:].rearrange("p t d -> p (t d)"), r=["vtok"], stream="dbg")
            chk("B1")
            tk.dma("sp", szh[:], sz_d[:, h * 128:(h + 1) * 128].rearrange("(t p) d -> p t d", p=128), r=[("sz_d", t_) for t_ in range(NT)], w=["vtok"], stream="szh")
            tmpo = o_dir[0]
            tk.op("dve", lambda e: e.tensor_tensor(out=tmpo[:], in0=o_dir[0][:], in1=o_dir[1][:], op=ALU.add), r=[("o_dir", 0), ("o_dir", 1)], w=["tmpo", ("o_dir", 0)])
            tk.op("pool", lambda e: e.tensor_tensor(out=o_dir[1][:], in0=tmpo[:], in1=tmpo[:], op=ALU.mult), r=["tmpo"], w=[("o_dir", 1)])
            tk.op("dve", lambda e: e.tensor_reduce(out=msn[:], in_=o_dir[1][:], axis=AX.X, op=ALU.add), r=[("o_dir", 1)], w=["msn"])
            tk.op("act", lambda e: e.activation(out=msn[:], in_=msn[:], func=ACT.Ln, bias=EPS, scale=1.0 / 128.0), r=["msn"], w=["msn"])
            tk.op("act", lambda e: e.activation(out=msn[:], in_=msn[:], func=ACT.Exp, scale=-0.5), r=["msn"], w=["msn"])
            tk.op("dve", lambda e: e.tensor_tensor(out=tmpo[:], in0=tmpo[:], in1=bc(msn[:, 0:NT], [[1, NT], [0, 128]]), op=ALU.mult), r=["tmpo", "msn"], w=["tmpo"])
            tk.op("dve", lambda e: e.tensor_tensor(out=tmpo[:], in0=tmpo[:], in1=bc(rows[:, R_DNW:R_DNW + 128], [[0, NT], [1, 128]]), op=ALU.mult), r=["tmpo", "rows"], w=["tmpo"])
            tk.op("dve", lambda e: e.tensor_tensor(out=dnb[:], in0=tmpo[:], in1=szh[:], op=ALU.mult), r=["tmpo", "vtok"], w=["ktok", ("o_dir", 0)])
            for t0 in range(0, NT, 8):
                pbt, pk = nb()
                for q8 in range(8):
                    tk.op("pe", lambda e, q8=q8, pbt=pbt, t0=t0: e.transpose(pbt[:, q8 * 128:(q8 + 1) * 128], dnb[:, t0 + q8, :], identb[:]), r=["ktok", "identb"], w=[pk])
                tk.op("act", lambda e, pbt=pbt, t0=t0: e.activation(out=dnT[:, t0 * 128:(t0 + 8) * 128], in_=pbt[:, :], func=ACT.Copy), r=[pk], w=["dnTb"])
            tk.dma("pool", mixT_d[4 + h, :, :], dnT[:], r=["dnTb"], w=[("mixT_d", "d", h)], stream="dnT")

    chk("B")
    tk.barrier()
    pgs.close()
    with ExitStack() as pc:
        stg = [sb(pc, "stgC%d" % i, [128, 1024], F32) for i in range(2)]
        w_out_b = sb(pc, "w_out_b", [128, 8, 1024], BF16)
        wq_b = sb(pc, "wq_b", [128, 8, 512], BF16)
        wkv_b = sb(pc, "wkv_b", [128, 8, 1024], BF16)
        wo_b = sb(pc, "wo_b", [128, 4, 1024], BF16)
        KmT = sb(pc, "KmT", [128, 2, 4, 256], BF16)
        Vx = sb(pc, "Vx", [128, 2, 2, 4, 129], BF16)
        mixT = [sb(pc, "mixT%d" % i, [128, 8, 512], BF16) for i in range(2)]
        x1 = sb(pc, "x1", [128, 4, D], F32)
        h2T = sb(pc, "h2T", [128, 8, 512], BF16)
        qxT = sb(pc, "qxT", [128, 4, 512], BF16)
        pTx = sb(pc, "pTx", [128, 8, 512], BF16)
        x2o = [sb(pc, "x2o%d" % i, [128, D], F32) for i in range(4)]
        xr4 = [sb(pc, "xr4_%d" % i, [128, D], F32) for i in range(4)]
        tmp4 = [sb(pc, "tmp4_%d" % i, [128, D], F32) for i in range(4)]
        hb4 = [sb(pc, "hb4_%d" % i, [128, D], BF16) for i in range(4)]
        oxb4 = [sb(pc, "oxb4_%d" % i, [128, 512], BF16) for i in range(4)]
        oxT4 = [sb(pc, "oxT4_%d" % i, [128, 4, 128], BF16) for i in range(4)]
        rdx4 = sb(pc, "rdx4", [128, 4, 4], F32)
        smc4 = sb(pc, "smc4", [128, 4, 4], F32)
        gp2 = sb(pc, "gp2", [128, 2048], F32)
        tk.dma("sp", gp2[:], rows_d[:, 0:2048], w=["gp2"], stream="c_pp")
        stgC = [(stg[i], ("stgC", i), "stgC%d" % i) for i in range(2)] + [(xr4[i], ("xr4", i), "stgC%d" % (2 + i)) for i in range(4)]
        load_weight_bf16(stgC, w_out_b, "w_out_b", w_out_d, 8, 1024)
        load_weight_bf16(stgC, wq_b, "wq_b", wq_d, 8, 512, gcol=P_GXA)
        load_weight_bf16(stgC, wkv_b, "wkv_b", wkv_d, 8, 1024, gcol=P_GMEM)
        load_weight_bf16(stgC, wo_b, "wo_b", wo_d, 4, 1024)
        tk.op("pool", lambda e: e.memset(Vx[:], 1.0), w=["Vx"])
        for sg in range(2):
            for kt in range(2):
                xs, xk = xr4[kt], ("xr4", kt)
                tk.dma("sp", xs[:], mem_d[sg, kt * 128:(kt + 1) * 128, :], w=[xk], stream="xr4_%d" % kt)
                norm_transpose(xs[:], xk, hb4[0][:], ("hb4", 0), h2T[:, :, kt * 128:(kt + 1) * 128], "h2T", tmp4[0][:], ("tmp4", 0))
            for hx in range(4):
                bank, bk = nf()
                for kc in range(8):
                    tk.op("pe", lambda e, kc=kc, bank=bank, hx=hx: e.matmul(bank[:, 0:256], lhsT=wkv_b[:, kc, hx * 128:(hx + 1) * 128], rhs=h2T[:, kc, 0:256],
                                                                          start=(kc == 0), stop=(kc == 7)), r=["wkv_b", "h2T"], w=[bk])
                tk.op("act", lambda e, bank=bank, hx=hx, sg=sg: e.activation(out=KmT[:, sg, hx, :], in_=bank[:, 0:256], func=ACT.Copy), r=[bk], w=["KmT"])
            for kt in range(2):
                bank, bk = nf()
                for kc in range(8):
                    tk.op("pe", lambda e, kc=kc, bank=bank, kt=kt: e.matmul(bank[:, 0:512], lhsT=h2T[:, kc, kt * 128:(kt + 1) * 128], rhs=wkv_b[:, kc, 512:1024],
                                                                          start=(kc == 0), stop=(kc == 7)), r=["wkv_b", "h2T"], w=[bk])
                tk.op("act", lambda e, bank=bank, kt=kt, sg=sg: e.activation(out=Vx[:, sg, kt, :, 0:128], in_=bank[:, 0:512].rearrange("p (a d) -> p a d", a=4), func=ACT.Copy),
                      r=[bk], w=["Vx"])

        freeC = list(range(6))

        def getC(n):
            while len(freeC) < n:
                yield
            out = []
            for _ in range(n):
                i = freeC.pop(0)
                out.append((pf[i], ("pf", i)))
            return out

        def relC(*bks):
            for bk in bks:
                freeC.append(bk[1])

        def run_chains(gens):
            active = list(gens)
            while active:
                still = []
                for g in active:
                    try:
                        next(g)
                        still.append(g)
                    except StopIteration:
                        pass
                active = still

        def post_norm_gen(bl, grow, growkey, res_ap, reskey, out_ap, outkey, tmp, tmpkey, smt, smk, add_eng="pool"):
            (b0, k0_), (b1, k1_) = bl
            banks, bkeys = [b0, b1], [k0_, k1_]
            for hf in range(2):
                tk.op("act", lambda e, hf=hf: e.activation(out=tmp[:, 0:512], in_=banks[hf][:, 0:512], func=ACT.Square, scale=1.0 / 32.0,
                                                           accum_out=smt[:, hf:hf + 1]), r=[bkeys[hf]], w=[tmpkey, smk])
            yield
            tk.op("dve", lambda e: e.tensor_tensor(out=smt[:, 0:1], in0=smt[:, 0:1], in1=smt[:, 1:2], op=ALU.add), r=[smk], w=[smk])
            yield
            tk.op("act", lambda e: e.activation(out=smt[:, 0:1], in_=smt[:, 0:1], func=ACT.Ln, bias=EPS, scale=1.0), r=[smk], w=[smk])
            tk.op("act", lambda e: e.activation(out=smt[:, 0:1], in_=smt[:, 0:1], func=ACT.Exp, scale=-0.5), r=[smk], w=[smk])
            yield
            for hf in range(2):
                tk.op("dve", lambda e, hf=hf: e.scalar_tensor_tensor(out=tmp[:, hf * 512:(hf + 1) * 512], in0=banks[hf][:, 0:512], scalar=smt[:, 0:1],
                                                                     in1=grow[:, hf * 512:(hf + 1) * 512], op0=ALU.mult, op1=ALU.mult),
                      r=[bkeys[hf], smk, growkey], w=[tmpkey])
            relC(k0_, k1_)
            yield
            tk.op(add_eng, lambda e: e.tensor_tensor(out=out_ap, in0=tmp, in1=res_ap, op=ALU.add), r=[tmpkey, reskey], w=[outkey])
            yield

        def norm_transpose_gen(xs, xkey, hb, hbkey, hT_out, hTkey, junk, junkkey, smt, smk):
            tk.op("act", lambda e: e.activation(out=junk, in_=xs, func=ACT.Square, scale=1.0 / 32.0, accum_out=smt[:, 2:3]), r=[xkey], w=[junkkey, smk])
            yield
            tk.op("act", lambda e: e.activation(out=smt[:, 2:3], in_=smt[:, 2:3], func=ACT.Ln, bias=EPS, scale=1.0), r=[smk], w=[smk])
            tk.op("act", lambda e: e.activation(out=smt[:, 2:3], in_=smt[:, 2:3], func=ACT.Exp, scale=-0.5), r=[smk], w=[smk])
            yield
            tk.op("dve", lambda e: e.tensor_scalar(out=hb, in0=xs, scalar1=smt[:, 2:3], scalar2=None, op0=ALU.mult), r=[xkey, smk], w=[hbkey])
            yield
            pbt, pk = nb()
            for kc in range(8):
                tk.op("pe", lambda e, kc=kc: e.transpose(pbt[:, kc * 128:(kc + 1) * 128], hb[:, kc * 128:(kc + 1) * 128], identb[:]), r=[hbkey, "identb"], w=[pk])
            tk.op("act", lambda e: e.activation(out=hT_out, in_=pbt[:, :].rearrange("p (k t) -> p k t", k=8), func=ACT.Copy), r=[pk], w=[hTkey])
            yield

        for mt in range(NMT):
            sg = (mt * 4) // SEG
            mx, mxk = mixT[mt % 2], ("mixT", mt % 2)
            tk.dma("sp", mx[:], mixT_d[:, :, mt * 512:(mt + 1) * 512].rearrange("k p t -> p k t"),
                   r=[("mixT_d", "a", j_) for j_ in range(mt * 4, mt * 4 + 4)] + [("mixT_d", "d", h_) for h_ in range(4)], w=[mxk], stream="mixT%d" % (mt % 2))

            def c1(s):
                t = mt * 4 + s
                xs, xk = xr4[s], ("xr4", s)
                tk.dma("sp", xs[:], x_d[t * 128:(t + 1) * 128, :], w=[xk], stream="xr4_%d" % s)
                bl = yield from getC(2)
                for hf in range(2):
                    bank, bk = bl[hf]
                    for kc in range(8):
                        tk.op("pe", lambda e, kc=kc, bank=bank, hf=hf: e.matmul(bank[:, 0:512], lhsT=mx[:, kc, s * 128:(s + 1) * 128], rhs=w_out_b[:, kc, hf * 512:(hf + 1) * 512],
                                                                              start=(kc == 0), stop=(kc == 7)), r=[mxk, "w_out_b"], w=[bk])
                yield
                yield from post_norm_gen(bl, gp2[:, 0:1024], "gp2", xs[:], xk, x1[:, s, :], ("x1", s), tmp4[s][:], ("tmp4", s), smc4[:, s, :], ("smc4", s))
                yield from norm_transpose_gen(x1[:, s, :], ("x1", s), hb4[s][:], ("hb4", s), h2T[:, :, s * 128:(s + 1) * 128], "h2T", tmp4[s][:], ("tmp4", s),
                                              smc4[:, s, :], ("smc4", s))
            run_chains([c1(s) for s in range(4)])

            for hx in range(4):
                bank, bk = nf()
                for kc in range(8):
                    tk.op("pe", lambda e, kc=kc, bank=bank, hx=hx: e.matmul(bank[:, 0:512], lhsT=wq_b[:, kc, hx * 128:(hx + 1) * 128], rhs=h2T[:, kc, :],
                                                                          start=(kc == 0), stop=(kc == 7)), r=["wq_b", "h2T"], w=[bk])
                tk.op("act", lambda e, bank=bank, hx=hx: e.activation(out=qxT[:, hx, :], in_=bank[:, 0:512], func=ACT.Copy, scale=128.0 ** -0.5), r=[bk], w=["qxT"])
            for hx in range(4):
                for kt in range(2):
                    bank, bk = nf()
                    tk.op("pe", lambda e, bank=bank, hx=hx, kt=kt: e.matmul(bank[:, 0:512], lhsT=KmT[:, sg, hx, kt * 128:(kt + 1) * 128], rhs=qxT[:, hx, :], start=True, stop=True),
                          r=["KmT", "qxT"], w=[bk])
                    tk.op("act", lambda e, bank=bank, hx=hx, kt=kt: e.activation(out=pTx[:, hx * 2 + kt, :], in_=bank[:, 0:512], func=ACT.Exp), r=[bk], w=["pTx"])

            def c3(s):
                t = mt * 4 + s
                ox_, oxk = oxb4[s], ("oxb4", s)
                oT_, oTk = oxT4[s], ("oxT4", s)
                bl = yield from getC(2)
                for hp in range(2):
                    bo, bok = bl[hp]
                    for hh in range(2):
                        hx = hp * 2 + hh
                        for kt in range(2):
                            tk.op("pe", lambda e, bo=bo, hx=hx, hh=hh, kt=kt: e.matmul(bo[:, hh * 129:(hh + 1) * 129], lhsT=pTx[:, hx * 2 + kt, s * 128:(s + 1) * 128],
                                                                                     rhs=Vx[:, sg, kt, hx, :], start=(kt == 0), stop=(kt == 1)), r=["pTx", "Vx"], w=[bok])
                yield
                for hp in range(2):
                    bo, bok = bl[hp]
                    bo3 = bo[:, 0:258].rearrange("p (i c) -> p i c", i=2)
                    rd = rdx4[:, s, hp * 2:(hp + 1) * 2]
                    tk.op("dve", lambda e, bo3=bo3, rd=rd: e.reciprocal(out=rd, in_=bo3[:, :, 128]), r=[bok], w=[("rdx4", s)])
                    tk.op("dve", lambda e, bo3=bo3, rd=rd, hp=hp: e.tensor_tensor(out=ox_[:, hp * 256:(hp + 1) * 256].rearrange("p (i d) -> p i d", i=2), in0=bo3[:, :, 0:128],
                                                                                  in1=bc(rd, [[1, 2], [0, 128]]), op=ALU.mult), r=[bok, ("rdx4", s)], w=[oxk])
                relC(bl[0][1], bl[1][1])
                yield
                pbt, pk = nb()
                for kc in range(4):
                    tk.op("pe", lambda e, kc=kc: e.transpose(pbt[:, kc * 128:(kc + 1) * 128], ox_[:, kc * 128:(kc + 1) * 128], identb[:]), r=[oxk, "identb"], w=[pk])
                tk.op("act", lambda e: e.activation(out=oT_[:], in_=pbt[:, 0:512].rearrange("p (k t) -> p k t", k=4), func=ACT.Copy), r=[pk], w=[oTk])
                yield
                bl2 = yield from getC(2)
                for hf in range(2):
                    bank, bk = bl2[hf]
                    for kc in range(4):
                        tk.op("pe", lambda e, kc=kc, bank=bank, hf=hf: e.matmul(bank[:, 0:512], lhsT=oT_[:, kc, :], rhs=wo_b[:, kc, hf * 512:(hf + 1) * 512],
                                                                              start=(kc == 0), stop=(kc == 3)), r=[oTk, "wo_b"], w=[bk])
                yield
                xo, xok = x2o[s], ("x2o", s)
                yield from post_norm_gen(bl2, gp2[:, 1024:2048], "gp2", x1[:, s, :], ("x1", s), xo[:], xok, tmp4[s][:], ("tmp4", s), smc4[:, s, :], ("smc4", s))
                tk.dma("pool", x2_d[t * 128:(t + 1) * 128, :], xo[:], r=[xok], w=[("x2_d", t)], stream="x2o%d" % s)
            run_chains([c3(s) for s in range(4)])

    chk("Ca")
    tk.barrier()
    with ExitStack() as pm:
        w1_b = sb(pm, "w1_b", [128, 8, 4096], BF16)
        w2_b = sb(pm, "w2_b", [128, 32, 1024], BF16)
        gp1 = sb(pm, "gp1", [128, 1024], F32)
        xm = [sb(pm, "xm%d" % i, [128, D], F32) for i in range(4)]
        hbm2 = [sb(pm, "hbm%d" % i, [128, D], BF16) for i in range(2)]
        junkm = sb(pm, "junkm", [128, D], BF16)
        h3T = sb(pm, "h3T", [128, 8, 256], BF16)
        uT = sb(pm, "uT", [128, 32, 256], BF16)
        rl = [sb(pm, "rl%d" % i, [128, 256], BF16) for i in range(2)]
        tmpm = sb(pm, "tmpm", [128, D], F32)
        yo = [sb(pm, "yo%d" % i, [128, D], F32) for i in range(2)]
        stg6 = [(yo[0], ("yo", 0)), (yo[1], ("yo", 1))] + [(xm[i], ("xm", i)) for i in range(4)]
        stgn = [0]

        def next_stg():
            i = stgn[0] % 6
            stgn[0] += 1
            return stg6[i][0], stg6[i][1], "stgM%d" % i
        tk.dma("sp", gp1[:], rows_d[:, 2048:3072], w=["gp1"], stream="c_rows")
        for kc in range(8):
            g = pp[:, P_GMLP + kc:P_GMLP + kc + 1]
            for q in range(4):
                sl, skey, sname = next_stg()
                tk.dma("sp", sl[:], w1_d[kc * 128:(kc + 1) * 128, q * 1024:(q + 1) * 1024], w=[skey], stream=sname)
                if q % 2 == 0:
                    tk.op("act", lambda e, g=g, sl=sl, kc=kc, q=q: e.activation(out=w1_b[:, kc, q * 1024:(q + 1) * 1024], in_=sl[:], func=ACT.Copy, scale=g),
                          r=[skey, "pp"], w=["w1_b"])
                else:
                    tk.op("dve", lambda e, g=g, sl=sl, kc=kc, q=q: e.tensor_scalar(out=w1_b[:, kc, q * 1024:(q + 1) * 1024], in0=sl[:], scalar1=g, scalar2=None, op0=ALU.mult),
                          r=[skey, "pp"], w=["w1_b"])
        for ft in range(32):
            sl, skey, sname = next_stg()
            tk.dma("sp", sl[:], w2_d[ft * 128:(ft + 1) * 128, :], w=[skey], stream=sname)
            if ft % 2 == 0:
                tk.op("act", lambda e, sl=sl, ft=ft: e.activation(out=w2_b[:, ft, :], in_=sl[:], func=ACT.Copy), r=[skey], w=["w2_b"])
            else:
                tk.op("dve", lambda e, sl=sl, ft=ft: e.tensor_copy(out=w2_b[:, ft, :], in_=sl[:]), r=[skey], w=["w2_b"])
        chk("Wm")
        def mlp_norm(mt_):
            for s in range(2):
                t = mt_ * 2 + s
                xi = (mt_ % 2) * 2 + s
                xs, xk = xm[xi], ("xm", xi)
                tk.dma("sp", xs[:], x2_d[t * 128:(t + 1) * 128, :], r=[("x2_d", t)], w=[xk], stream="xm%d" % xi)
                norm_part(xs[:], xk, hbm2[s][:], ("hbm", s), junkm[:], "junkm")

        mlp_norm(0)
        for mt in range(NT // 2):
            for s in range(2):
                transpose_part(hbm2[s][:], ("hbm", s), h3T[:, :, s * 128:(s + 1) * 128], "h3T")
            for ft in range(32):
                bank, bk = nf()
                for kc in range(8):
                    tk.op("pe", lambda e, kc=kc, bank=bank, ft=ft: e.matmul(bank[:, 0:256], lhsT=w1_b[:, kc, ft * 128:(ft + 1) * 128], rhs=h3T[:, kc, :],
                                                                          start=(kc == 0), stop=(kc == 7)), r=["w1_b", "h3T"], w=[bk])
                r_, rk = rl[ft % 2], ("rl", ft % 2)
                tk.op("act", lambda e, bank=bank, r_=r_: e.activation(out=r_[:], in_=bank[:, 0:256], func=ACT.Relu), r=[bk], w=[rk])
                en = "dve" if ft % 2 == 0 else "pool"
                tk.op(en, lambda e, r_=r_, ft=ft: e.tensor_tensor(out=uT[:, ft, :], in0=r_[:], in1=r_[:], op=ALU.mult), r=[rk], w=["uT"])
            if mt + 1 < NT // 2:
                mlp_norm(mt + 1)
            for s in range(2):
                t = mt * 2 + s
                banks, bkeys = [], []
                for hf in range(2):
                    bank, bk = nf()
                    for ft in range(32):
                        tk.op("pe", lambda e, ft=ft, bank=bank, hf=hf: e.matmul(bank[:, 0:512], lhsT=uT[:, ft, s * 128:(s + 1) * 128], rhs=w2_b[:, ft, hf * 512:(hf + 1) * 512],
                                                                              start=(ft == 0), stop=(ft == 31)), r=["uT", "w2_b"], w=[bk])
                    banks.append(bank); bkeys.append(bk)
                yo_, yok = yo[t % 2], ("yo", t % 2)
                xi = (mt % 2) * 2 + s
                post_norm_residual(banks, bkeys, gp1[:], "gp1", xm[xi][:], ("xm", xi), yo_[:], yok, junkm[:], "junkm", tmpm[:], "tmpm")
                tk.dma("pool", y_d[t * 128:(t + 1) * 128, :], yo_[:], r=[yok], stream="yo%d" % (t % 2))

    tk.finish("pool")
    es.close()


def make_consts(NT, SEG, pos):
    c = np.zeros((128, 2816), np.float32)
    idx = np.arange(128)
    same = (idx[:, None] // 64) == (idx[None, :] // 64)
    r_, c_ = idx[:, None], idx[None, :]
    mats = [
        np.eye(128),
        same & (r_ <= c_),
        same & (r_ > c_),
        same & (r_ >= c_),
        same & (r_ < c_),
        -1.0 * (same & (c_ < r_)),
        -1.0 * (same & (c_ > r_)),
        same & (r_ <= c_),
        same & (r_ >= c_),
        np.broadcast_to(idx[:, None] < 64, (128, 128)),
        np.broadcast_to(idx[:, None] >= 64, (128, 128)),
        np.ones((128, 128)),
    ]
    prot = np.zeros((128, 128))
    for m in range(128):
        mm = m % 64
        if mm < 8:
            prot[m + 8, m] = -1.0
        elif mm < 16:
            prot[m - 8, m] = 1.0
    mats.append(prot)
    for i, m in enumerate(mats):
        c[:, i * 128:(i + 1) * 128] = np.asarray(m, np.float32)
    mprev = (r_ >= c_).astype(np.float32)
    mnext = (r_ <= c_).astype(np.float32)
    c[:, 13 * 128:13 * 128 + 512] = np.tile(mprev, (1, 4))
    c[:, 13 * 128 + 512:13 * 128 + 1024] = np.tile(mnext, (1, 4))
    T = NT * 128
    half = 8
    inv = (500000.0 ** (-(np.arange(half, dtype=np.float32) * 2.0 / 16.0))).astype(np.float32)
    ang = pos.astype(np.float32)[:, None] * inv[None, :]
    cos = np.cos(ang).astype(np.float32)
    sin = np.sin(ang).astype(np.float32)
    cs = np.zeros((128, 2 * T), np.float32)
    cs[:, :T] = 1.0
    for p in range(128):
        mm = p % 64
        if mm < 16:
            cs[p, :T] = cos[:, mm % 8]
            cs[p, T:] = sin[:, mm % 8]
    return c, cs


def make_pp(inp, flagval):
    pp = np.zeros((128, 96), np.float32)
    for off, name in ((0, "norm_pre_mix"), (8, "norm_pre_xa"), (16, "norm_pre_mlp"), (24, "mem_norm_g")):
        pp[:, off:off + 8] = np.asarray(inp[name], np.float32).reshape(8, 128).T
    cw = np.asarray(inp["dn_conv_w"], np.float32).reshape(5, 12, 128)
    pp[:, 32:92] = cw.transpose(2, 1, 0).reshape(128, 60)
    pp[:, 92] = flagval
    return pp


def make_rows(inp):
    r = np.zeros((3232,), np.float32)
    r[0:1024] = np.asarray(inp["norm_post_mix"]).reshape(-1)
    r[1024:2048] = np.asarray(inp["norm_post_xa"]).reshape(-1)
    r[2048:3072] = np.asarray(inp["norm_post_mlp"]).reshape(-1)
    r[3072:3200] = np.asarray(inp["dn_norm_w"]).reshape(-1)
    r[3200:3208] = np.asarray(inp["attn_sink"]).reshape(-1)
    r[3208:3212] = np.asarray(inp["dn_A_log_f"]).reshape(-1)
    r[3212:3216] = np.asarray(inp["dn_A_log_b"]).reshape(-1)
    r[3216:3220] = np.asarray(inp["dn_dt_bias_f"]).reshape(-1)
    r[3220:3224] = np.asarray(inp["dn_dt_bias_b"]).reshape(-1)
    return np.ascontiguousarray(np.broadcast_to(r[None, :], (128, 3232)))


def run_cores(inp, core_specs, NT, SEG, stop=None):
    nc = build(NT, SEG, stop)
    rows = make_rows(inp)
    f32 = lambda a: np.ascontiguousarray(np.asarray(a, np.float32))
    shared = {
        "w_in": f32(inp["w_in"][0]), "w_out": f32(inp["w_out"][0]), "xa_wq": f32(inp["xa_wq"][0]),
        "xa_wkv": f32(inp["xa_wkv"][0]), "xa_wo": f32(inp["xa_wo"][0]), "mlp_w1": f32(inp["mlp_w1"][0]),
        "mlp_w2": f32(inp["mlp_w2"][0]), "rows": rows,
    }
    in_maps = []
    for (x, mem, flagval, pos) in core_specs:
        cst, cs = make_consts(NT, SEG, pos)
        m = dict(shared)
        m.update({"x": f32(x), "mem": f32(mem), "pp": make_pp(inp, flagval), "cst": cst, "cossin": cs})
        in_maps.append(m)
    res = run_bass_kernel_spmd(nc, in_maps, core_ids=list(range(len(core_specs))))
    if DEBUG:
        for i, r in enumerate(res.results):
            DBG_OUT[i] = {k: np.asarray(v) for k, v in r.items()}
    return [r["y"] for r in res.results]


def kernel(**inputs):
    inp = {k: np.asarray(v) for k, v in inputs.items()}
    NT, SEG = 32, 16
    xp, xs = inp["x_prompt"], inp["x_sample"]
    mp, ms = inp["mem_prompt"], inp["mem_sample"]
    specs = []
    for b in range(4):
        specs.append((xp[b], np.stack([mp[b], mp[b]]), 1.0, np.arange(4096)))
    for c in range(4):
        specs.append((np.concatenate([xs[2 * c], xs[2 * c + 1]], 0), np.stack([ms[2 * c], ms[2 * c + 1]]), 0.0,
                      np.concatenate([np.arange(2048), np.arange(2048)])))
    ys = run_cores(inp, specs, NT, SEG)
    y_prompt = np.stack([ys[b] for b in range(4)]).astype(np.float32)
    y_sample = np.stack([ys[4 + c][sg * 2048:(sg + 1) * 2048] for c in range(4) for sg in range(2)]).astype(np.float32)
    return (y_prompt, y_sample)
```

```python
import numpy as np
import concourse.bass as bass
import concourse.mybir as mybir
from concourse.bass_utils import run_bass_kernel_spmd

F32 = mybir.dt.float32
BF16 = mybir.dt.bfloat16
ACT = mybir.ActivationFunctionType
ALU = mybir.AluOpType
AX = mybir.AxisListType

D = 1024
IN_W = 2832
EPS = 1e-6


class TK:
    def __init__(self, nc, sems):
        self.nc = nc
        self.sems = sems
        self.eng = {}
        for name, obj in (("pe", nc.tensor), ("act", nc.scalar), ("dve", nc.vector),
                          ("pool", nc.gpsimd), ("sp", nc.sync)):
            self.eng[name] = dict(obj=obj, sem=self.sems.pop(), cnt=0, known={})
        self.state = {}
        self.streams = {}
        self.strict = {"pe": False, "act": True, "dve": True, "pool": True, "sp": False}

    def _deps(self, r, w):
        deps = []
        for k in r:
            st = self.state.get(k)
            if st is not None and st["w"] is not None:
                deps.append(st["w"])
            if st is not None and isinstance(k, tuple) and k[0] in ("pf", "pb"):
                deps.extend(st["r"].values())
        for k in w:
            st = self.state.get(k)
            if st is not None:
                if st["w"] is not None:
                    deps.append(st["w"])
                deps.extend(st["r"].values())
        return deps

    def _wait(self, ename, deps):
        E = self.eng[ename]
        need = {}
        for (sem, val, src) in deps:
            if src == ename and not self.strict[ename]:
                continue
            key = id(sem)
            if E["known"].get(key, 0) >= val:
                continue
            if key not in need or need[key][1] < val:
                need[key] = (sem, val)
        for key, (sem, val) in need.items():
            E["obj"].wait_ge(sem, val)
            E["known"][key] = val

    def _book(self, r, w, tok, rkey):
        for k in r:
            st = self.state.get(k)
            if st is None:
                st = self.state[k] = {"w": None, "r": {}}
            st["r"][rkey] = tok
        for k in w:
            self.state[k] = {"w": tok, "r": {}}

    def op(self, ename, fn, r=(), w=()):
        self._wait(ename, self._deps(r, w))
        E = self.eng[ename]
        E["cnt"] += 1
        fn(E["obj"]).then_inc(E["sem"], 1)
        self._book(r, w, (E["sem"], E["cnt"], ename), ename)

    def dma(self, qname, out, in_, r=(), w=(), stream="d"):
        self._wait(qname, self._deps(r, w))
        if stream not in self.streams:
            self.streams[stream] = [self.sems.pop(), 0]
        S = self.streams[stream]
        S[1] += 1
        self.eng[qname]["obj"].dma_start(out=out, in_=in_).then_inc(S[0], 16)
        self._book(r, w, (S[0], 16 * S[1], "dma:" + stream), "dma:" + stream)

    def barrier(self):
        for ename, E in self.eng.items():
            for name, S in self.streams.items():
                if S[1] > 0 and E["known"].get(id(S[0]), 0) < 16 * S[1]:
                    E["obj"].wait_ge(S[0], 16 * S[1])
                    E["known"][id(S[0])] = 16 * S[1]
            for name, o in self.eng.items():
                if name != ename and o["cnt"] > 0 and E["known"].get(id(o["sem"]), 0) < o["cnt"]:
                    E["obj"].wait_ge(o["sem"], o["cnt"])
                    E["known"][id(o["sem"])] = o["cnt"]

    def finish(self, ename="pool"):
        E = self.eng[ename]
        for name, S in self.streams.items():
            if S[1] > 0:
                E["obj"].wait_ge(S[0], 16 * S[1])
        for name, o in self.eng.items():
            if name != ename and o["cnt"] > 0:
                E["obj"].wait_ge(o["sem"], o["cnt"])


def bc(ap, dims):
    return bass.AP(ap.tensor, ap.offset, [list(ap.ap[0])] + [list(d) for d in dims])


DEBUG = False
DBG_OUT = {}


class _Stop(Exception):
    pass


def build(NT, SEG, stop=None):
    nc_box = {}
    try:
        _build(NT, SEG, stop, nc_box)
    except _Stop:
        nc_box["tk"].finish("pool")
    return nc_box["nc"]


def _build(NT, SEG, stop, nc_box):
    T = NT * 128
    SEGT = SEG * 128
    PADW = SEGT + 4
    NMT = NT // 4
    nc = bass.Bass("TRN2", target_bir_lowering=False)
    nc_box["nc"] = nc

    def chk(name):
        if stop == name:
            raise _Stop()

    def din(name, shape, dt=F32):
        return nc.dram_tensor(name, shape, dt, kind="ExternalInput").ap()

    x_d = din("x", [T, D])
    mem_d = din("mem", [2, 256, D])
    w_in_d = din("w_in", [D, IN_W])
    w_out_d = din("w_out", [D, D])
    wq_d = din("xa_wq", [D, 512])
    wkv_d = din("xa_wkv", [D, 1024])
    wo_d = din("xa_wo", [512, D])
    w1_d = din("mlp_w1", [D, 4096])
    w2_d = din("mlp_w2", [4096, D])
    pp_d = din("pp", [128, 96])
    rows_d = din("rows", [128, 3232])
    cst_d = din("cst", [128, 2816])
    cs_d = din("cossin", [128, 2 * T])
    y_d = nc.dram_tensor("y", [T, D], F32, kind="ExternalOutput").ap()

    kw = {"kind": "ExternalOutput"} if DEBUG else {}
    dqkv_d = nc.dram_tensor("dqkv_s", [12, 128, T], BF16, **kw).ap()
    sz_d = nc.dram_tensor("sz_s", [T, 512], BF16, **kw).ap()
    mixT_d = nc.dram_tensor("mixT_s", [8, 128, T], BF16, **kw).ap()
    x2_d = nc.dram_tensor("x2_s", [T, D], F32, **kw).ap()

    from contextlib import ExitStack
    es = ExitStack()
    sems = [es.enter_context(nc.semaphore("s%d" % i)) for i in range(60)]
    tk = TK(nc, sems)
    nc_box["tk"] = tk

    def sb(stack, name, shape, dt):
        return stack.enter_context(nc.sbuf_tensor("sb_" + name, shape, dt))

    pf = [es.enter_context(nc.psum_tensor("pf%d" % i, [128, 512], F32)) for i in range(6)]
    pb = [es.enter_context(nc.psum_tensor("pb%d" % i, [128, 1024], BF16)) for i in range(2)]
    rr = {"f": 0, "b": 0}

    def nf():
        i = rr["f"]; rr["f"] = (i + 1) % 6
        return pf[i], ("pf", i)

    def nb():
        i = rr["b"]; rr["b"] = (i + 1) % 2
        return pb[i], ("pb", i)

    cst = sb(es, "cst", [128, 2816], F32)
    pp = sb(es, "pp", [128, 96], F32)
    rows = sb(es, "rows", [128, 160], F32)
    identb = sb(es, "identb", [128, 128], BF16)
    onesb = sb(es, "onesb", [128, 128], BF16)
    tk.dma("sp", cst[:], cst_d[:, :], w=["cst"], stream="c_cst")
    tk.dma("sp", pp[:], pp_d[:, :], w=["pp"], stream="c_pp")
    tk.dma("sp", rows[:], rows_d[:, 3072:3232], w=["rows"], stream="c_rows")
    C_ID, C_MIF, C_MSF, C_MIB, C_MSB, C_NSF, C_NSB, C_MTF, C_MTB, C_I0, C_I1, C_ONE, C_PROT = [i * 128 for i in range(13)]
    C_MPREV = 13 * 128
    C_MNEXT = C_MPREV + 512
    ident = cst[:, C_ID:C_ID + 128]
    tk.op("dve", lambda e: e.tensor_copy(out=identb[:], in_=ident), r=["cst"], w=["identb"])
    tk.op("dve", lambda e: e.tensor_copy(out=onesb[:], in_=cst[:, C_ONE:C_ONE + 128]), r=["cst"], w=["onesb"])
    P_GMIX, P_GXA, P_GMLP, P_GMEM, P_CONV, P_FLAG = 0, 8, 16, 24, 32, 92
    flag = pp[:, P_FLAG:P_FLAG + 1]
    R_DNW, R_SINK, R_ALOG, R_DT = 0, 128, 136, 144

    sm = sb(es, "sm", [128, 64], F32)
    smc = {"i": 0}

    def smcol(n=1):
        i = smc["i"]
        if i + n > 64:
            i = 0
        smc["i"] = i + n
        return sm[:, i:i + n], ("sm", i // 8)

    def rstd_from_ms(ms_ap, mskey, out_ap, outkey, n_eps=EPS):
        tk.op("act", lambda e: e.activation(out=out_ap, in_=ms_ap, func=ACT.Ln, bias=n_eps, scale=1.0),
              r=[mskey], w=[outkey])
        tk.op("act", lambda e: e.activation(out=out_ap, in_=out_ap, func=ACT.Exp, scale=-0.5),
              r=[outkey], w=[outkey])

    lw_cnt = [0]

    def load_weight_bf16(stack_stage, dst, dstkey, src_d, nk, ncols, gcol=None, post=None, engs=("act", "dve")):
        stg = stack_stage
        for kc in range(nk):
            sl, skey, sname = stg[lw_cnt[0] % len(stg)]
            lw_cnt[0] += 1
            tk.dma("sp", sl[:, 0:ncols], src_d[kc * 128:(kc + 1) * 128, :], w=[skey], stream=sname)
            en = engs[kc % len(engs)]
            if post is not None:
                post(kc, sl, skey, en)
                continue
            if gcol is None:
                if en == "act":
                    tk.op("act", lambda e, kc=kc, sl=sl: e.activation(out=dst[:, kc, :], in_=sl[:, 0:ncols], func=ACT.Copy),
                          r=[skey], w=[dstkey])
                else:
                    tk.op(en, lambda e, kc=kc, sl=sl: e.tensor_copy(out=dst[:, kc, :], in_=sl[:, 0:ncols]),
                          r=[skey], w=[dstkey])
            else:
                g = pp[:, gcol + kc:gcol + kc + 1]
                if en == "act":
                    tk.op("act", lambda e, kc=kc, sl=sl, g=g: e.activation(out=dst[:, kc, :], in_=sl[:, 0:ncols], func=ACT.Copy, scale=g),
                          r=[skey, "pp"], w=[dstkey])
                else:
                    tk.op(en, lambda e, kc=kc, sl=sl, g=g: e.tensor_scalar(out=dst[:, kc, :], in0=sl[:, 0:ncols], scalar1=g, scalar2=None, op0=ALU.mult),
                          r=[skey, "pp"], w=[dstkey])

    def norm_part(xs, xkey, hb, hbkey, junk, junkkey):
        ms, mk = smcol()
        tk.op("act", lambda e: e.activation(out=junk, in_=xs, func=ACT.Square, scale=1.0 / 32.0, accum_out=ms),
              r=[xkey], w=[junkkey, mk])
        rstd_from_ms(ms, mk, ms, mk)
        tk.op("dve", lambda e: e.tensor_scalar(out=hb, in0=xs, scalar1=ms, scalar2=None, op0=ALU.mult),
              r=[xkey, mk], w=[hbkey])

    def transpose_part(hb, hbkey, hT_out, hTkey):
        pbt, pk = nb()
        for kc in range(8):
            tk.op("pe", lambda e, kc=kc: e.transpose(pbt[:, kc * 128:(kc + 1) * 128], hb[:, kc * 128:(kc + 1) * 128], identb[:]),
                  r=[hbkey, "identb"], w=[pk])
        tk.op("act", lambda e: e.activation(out=hT_out, in_=pbt[:, :].rearrange("p (k t) -> p k t", k=8), func=ACT.Copy),
              r=[pk], w=[hTkey])

    def norm_transpose(xs, xkey, hb, hbkey, hT_out, hTkey, junk, junkkey):
        ms, mk = smcol()
        tk.op("act", lambda e: e.activation(out=junk, in_=xs, func=ACT.Square, scale=1.0 / 32.0, accum_out=ms),
              r=[xkey], w=[junkkey, mk])
        rstd_from_ms(ms, mk, ms, mk)
        tk.op("dve", lambda e: e.tensor_scalar(out=hb, in0=xs, scalar1=ms, scalar2=None, op0=ALU.mult),
              r=[xkey, mk], w=[hbkey])
        pbt, pk = nb()
        for kc in range(8):
            tk.op("pe", lambda e, kc=kc: e.transpose(pbt[:, kc * 128:(kc + 1) * 128], hb[:, kc * 128:(kc + 1) * 128], identb[:]),
                  r=[hbkey, "identb"], w=[pk])
        tk.op("act", lambda e: e.activation(out=hT_out, in_=pbt[:, :].rearrange("p (k t) -> p k t", k=8), func=ACT.Copy),
              r=[pk], w=[hTkey])

    def post_norm_residual(banks, bkeys, grow, growkey, res_ap, reskey, out_ap, outkey, junk, junkkey, tmp, tmpkey, add_eng="pool"):
        m0, k0 = smcol(2)
        for hf in range(2):
            tk.op("act", lambda e, hf=hf: e.activation(out=junk[:, 0:512], in_=banks[hf][:, 0:512], func=ACT.Square, scale=1.0 / 32.0,
                                                       accum_out=m0[:, hf:hf + 1]),
                  r=[bkeys[hf]], w=[junkkey, k0])
        tk.op("dve", lambda e: e.tensor_tensor(out=m0[:, 0:1], in0=m0[:, 0:1], in1=m0[:, 1:2], op=ALU.add), r=[k0], w=[k0])
        rstd_from_ms(m0[:, 0:1], k0, m0[:, 0:1], k0)
        for hf in range(2):
            tk.op("dve", lambda e, hf=hf: e.scalar_tensor_tensor(out=tmp[:, hf * 512:(hf + 1) * 512], in0=banks[hf][:, 0:512], scalar=m0[:, 0:1],
                                                                 in1=grow[:, hf * 512:(hf + 1) * 512],
                                                                 op0=ALU.mult, op1=ALU.mult),
                  r=[bkeys[hf], k0, growkey], w=[tmpkey])
        tk.op(add_eng, lambda e: e.tensor_tensor(out=out_ap, in0=tmp, in1=res_ap, op=ALU.add), r=[tmpkey, reskey], w=[outkey])

    pgs = ExitStack()
    beta = sb(pgs, "beta", [128, NT, 8], F32)
    gg = sb(pgs, "gg", [128, NT, 8], F32)
    gc = sb(pgs, "gc", [128, NT, 8], F32)
    glb = sb(pgs, "glb", [128, NT, 2, 8], F32)
    kds = sb(pgs, "kds", [128, NT, 8], F32)
    bg = sb(pgs, "bg", [128, NT, 8], F32)
    egc = sb(pgs, "egc", [128, NT, 8], F32)
    pq = ExitStack()
    qT = sb(pq, "qT", [128, NT, 4, 128], BF16)
    kT = sb(pq, "kT", [128, T], BF16)
    vaug = sb(pq, "vaug", [128, NT, 2, 65], BF16)
    gates = sb(pq, "gates", [128, NT, 16], F32)
    with ExitStack() as pa:
        stgA = [sb(pa, "stgA%d" % i, [128, 1416], F32) for i in range(4)]
        w_in_b = sb(pa, "w_in_b", [128, 8, IN_W], BF16)
        protb = sb(pa, "protb", [128, 128], BF16)
        csb = [sb(pa, "cs%d" % i, [128, 2, 512], F32) for i in range(2)]
        xb2 = [sb(pa, "xa%d" % i, [128, D], F32) for i in range(2)]
        hb4A = [sb(pa, "hbA%d" % i, [128, D], BF16) for i in range(4)]
        junk = sb(pa, "junkA", [128, D], BF16)
        hT = [sb(pa, "hTA%d" % i, [128, 8, 512], BF16) for i in range(2)]
        qraw = [sb(pa, "qraw%d" % i, [128, 512], BF16) for i in range(2)]
        t1 = [sb(pa, "t1_%d" % i, [128, 512], F32) for i in range(2)]
        t2 = [sb(pa, "t2_%d" % i, [128, 512], F32) for i in range(2)]
        spl = [sb(pa, "spl%d" % i, [128, 512], BF16) for i in range(3)]
        ez = [sb(pa, "ez%d" % i, [128, 512], F32) for i in range(2)]
        szb = [sb(pa, "szb%d" % i, [128, 512], BF16) for i in range(2)]

        tk.op("dve", lambda e: e.tensor_copy(out=protb[:], in_=cst[:, C_PROT:C_PROT + 128]), r=["cst"], w=["protb"])
        tk.op("pool", lambda e: e.memset(vaug[:], 1.0), w=["vaug"])

        for kc in range(8):
            g = pp[:, P_GMIX + kc:P_GMIX + kc + 1]
            for hv in range(2):
                si_ = (kc * 2 + hv) % 4
                sl, skey = stgA[si_], ("stgA", si_)
                tk.dma("sp", sl[:], w_in_d[kc * 128:(kc + 1) * 128, hv * 1416:(hv + 1) * 1416], w=[skey], stream="stgA%d" % si_)
                if hv == 0:
                    src_q = sl[:, 0:512].rearrange("p (a i d) -> p i a d", a=2, i=4)
                    dst_q = w_in_b[:, kc, 0:512].rearrange("p (i a d) -> p i a d", a=2, i=4)
                    tk.op("dve", lambda e, g=g, src_q=src_q, dst_q=dst_q: e.tensor_scalar(out=dst_q, in0=src_q, scalar1=g, scalar2=None, op0=ALU.mult),
                          r=[skey, "pp"], w=["w_in_b"])
                    tk.op("act", lambda e, g=g, sl=sl, kc=kc: e.activation(out=w_in_b[:, kc, 512:1416], in_=sl[:, 512:1416], func=ACT.Copy, scale=g),
                          r=[skey, "pp"], w=["w_in_b"])
                else:
                    tk.op("act", lambda e, g=g, sl=sl, kc=kc: e.activation(out=w_in_b[:, kc, 1416:2832], in_=sl[:, 0:1416], func=ACT.Copy, scale=g),
                          r=[skey, "pp"], w=["w_in_b"])

        chk("W")

        def a_norm(mt_):
            for s in range(4):
                t = mt_ * 4 + s
                xs = xb2[t % 2]
                xk = ("xa", t % 2)
                tk.dma("sp", xs[:], x_d[t * 128:(t + 1) * 128, :], w=[xk], stream="xa%d" % (t % 2))
                norm_part(xs[:], xk, hb4A[s][:], ("hbA", s), junk[:], "junkA")

        for mt in range(NMT):
            hTm = hT[mt % 2]
            hk = ("hTA", mt % 2)
            cs, csk = csb[mt % 2], ("cs", mt % 2)
            tk.dma("sp", cs[:, 0, :], cs_d[:, mt * 512:(mt + 1) * 512], w=[csk], stream="cs%d" % (mt % 2))
            tk.dma("sp", cs[:, 1, :], cs_d[:, T + mt * 512:T + (mt + 1) * 512], w=[csk], stream="cs%d" % (mt % 2))
            if mt == 0:
                a_norm(0)
            for s in range(4):
                transpose_part(hb4A[s][:], ("hbA", s), hTm[:, :, s * 128:(s + 1) * 128], hk)
            chk("A_nt")
            for ct in range(17):
                if ct == 5:
                    chk("A_q")
                c0 = ct * 128 if ct < 5 else 768 + (ct - 5) * 128
                bank, bk = nf()
                for kc in range(8):
                    tk.op("pe", lambda e, kc=kc, c0=c0: e.matmul(bank[:, 0:512], lhsT=w_in_b[:, kc, c0:c0 + 128], rhs=hTm[:, kc, :],
                                                                start=(kc == 0), stop=(kc == 7)),
                          r=["w_in_b", hk], w=[bk])
                chk("A_m0")
                if ct < 5:
                    i2 = ct % 2
                    qr, qk_ = qraw[i2], ("qraw", i2)
                    tk.op("act", lambda e, qr=qr: e.activation(out=qr[:], in_=bank[:, 0:512], func=ACT.Copy), r=[bk], w=[qk_])
                    bank2, bk2 = nf()
                    tk.op("pe", lambda e, qr=qr, bank2=bank2: e.matmul(bank2[:, 0:512], lhsT=protb[:], rhs=qr[:], start=True, stop=True),
                          r=["protb", qk_], w=[bk2])
                    chk("A_m1")
                    a1, a1k, a2, a2k = t1[i2], ("t1", i2), t2[i2], ("t2", i2)
                    tk.op("dve", lambda e, a1=a1: e.tensor_tensor(out=a1[:], in0=bank[:, 0:512], in1=cs[:, 0, :], op=ALU.mult),
                          r=[bk, csk], w=[a1k])
                    tk.op("dve", lambda e, a2=a2, bank2=bank2: e.tensor_tensor(out=a2[:], in0=bank2[:, 0:512], in1=cs[:, 1, :], op=ALU.mult),
                          r=[bk2, csk], w=[a2k])
                    chk("A_m2")
                    if ct < 4:
                        dst = qT[:, mt * 4:(mt + 1) * 4, ct, :]
                        tk.op("dve", lambda e, a1=a1, a2=a2, dst=dst: e.tensor_tensor(out=dst, in0=a1[:].rearrange("p (s t) -> p s t", s=4),
                                                                                       in1=a2[:].rearrange("p (s t) -> p s t", s=4), op=ALU.add),
                              r=[a1k, a2k], w=["qT"])
                    else:
                        tk.op("dve", lambda e, a1=a1, a2=a2: e.tensor_tensor(out=kT[:, mt * 512:(mt + 1) * 512], in0=a1[:], in1=a2[:], op=ALU.add),
                              r=[a1k, a2k], w=["kT"])
                else:
                    i3 = ct % 3
                    sp_, spk = spl[i3], ("spl", i3)
                    if ct % 2 == 0:
                        tk.op("act", lambda e, sp_=sp_: e.activation(out=sp_[:], in_=bank[:, 0:512], func=ACT.Copy), r=[bk], w=[spk])
                    else:
                        tk.op("dve", lambda e, sp_=sp_: e.tensor_copy(out=sp_[:], in_=bank[:, 0:512]), r=[bk], w=[spk])
                    tk.dma("pool", dqkv_d[ct - 5, :, mt * 512:(mt + 1) * 512], sp_[:], r=[spk], w=[("dqkv_d", ct - 5, mt)], stream="spl%d" % i3)
            chk("A_fm")
            if mt + 1 < NMT:
                a_norm(mt + 1)
            for s in range(4):
                t = mt * 4 + s
                bankA, bkA = nf()
                for kc in range(8):
                    tk.op("pe", lambda e, kc=kc: e.matmul(bankA[:, 0:128], lhsT=hTm[:, kc, s * 128:(s + 1) * 128], rhs=w_in_b[:, kc, 640:768],
                                                          start=(kc == 0), stop=(kc == 7)), r=["w_in_b", hk], w=[bkA])
                for kc in range(8):
                    tk.op("pe", lambda e, kc=kc: e.matmul(bankA[:, 128:144], lhsT=hTm[:, kc, s * 128:(s + 1) * 128], rhs=w_in_b[:, kc, 2816:2832],
                                                          start=(kc == 0), stop=(kc == 7)), r=["w_in_b", hk], w=[bkA])
                bankB, bkB = nf()
                for kc in range(8):
                    tk.op("pe", lambda e, kc=kc: e.matmul(bankB[:, 0:512], lhsT=hTm[:, kc, s * 128:(s + 1) * 128], rhs=w_in_b[:, kc, 2304:2816],
                                                          start=(kc == 0), stop=(kc == 7)), r=["w_in_b", hk], w=[bkB])
                tk.op("act", lambda e, t=t: e.activation(out=vaug[:, t, :, 0:64], in_=bankA[:, 0:128].rearrange("p (a d) -> p a d", a=2), func=ACT.Copy),
                      r=[bkA], w=["vaug"])
                tk.op("dve", lambda e, t=t: e.tensor_copy(out=gates[:, t, :], in_=bankA[:, 128:144]), r=[bkA], w=["gates"])
                e_, ek = ez[s % 2], ("ez", s % 2)
                z_, zk = szb[s % 2], ("szb", s % 2)
                tk.op("act", lambda e, e_=e_: e.activation(out=e_[:], in_=bankB[:, 0:512], func=ACT.Exp, scale=-1.0), r=[bkB], w=[ek])
                tk.op("act", lambda e, e_=e_: e.activation(out=e_[:], in_=e_[:], func=ACT.Ln, bias=1.0), r=[ek], w=[ek])
                tk.op("act", lambda e, e_=e_: e.activation(out=e_[:], in_=e_[:], func=ACT.Exp, scale=-1.0), r=[ek], w=[ek])
                tk.op("dve", lambda e, e_=e_, z_=z_: e.tensor_tensor(out=z_[:], in0=bankB[:, 0:512], in1=e_[:], op=ALU.mult), r=[bkB, ek], w=[zk])
                tk.dma("pool", sz_d[t * 128:(t + 1) * 128, :], z_[:], r=[zk], w=[("sz_d", t)], stream="szb%d" % (s % 2))

    chk("A")
    tk.barrier()
    if True:
        with ExitStack() as pa2:
            mprev = sb(pa2, "mprev", [128, 512], BF16)
            mnext = sb(pa2, "mnext", [128, 512], BF16)
            mprevf = sb(pa2, "mprevf", [128, 512], BF16)
            mnextf = sb(pa2, "mnextf", [128, 512], BF16)
            esink = sb(pa2, "esink", [128, 8], F32)
            pT = [sb(pa2, "pT%d" % i, [128, 512], BF16) for i in range(24)]
            attb = [sb(pa2, "attb%d" % i, [128, 512], BF16) for i in range(4)]
            attTs = [sb(pa2, "attTs%d" % i, [128, 512], BF16) for i in range(4)]
            den = sb(pa2, "den", [128, 4, 8], F32)
            tk.op("dve", lambda e: e.tensor_copy(out=mprev[:], in_=cst[:, C_MPREV:C_MPREV + 512]), r=["cst"], w=["mprev"])
            tk.op("dve", lambda e: e.tensor_copy(out=mnext[:], in_=cst[:, C_MNEXT:C_MNEXT + 512]), r=["cst"], w=["mnext"])
            tk.op("dve", lambda e: e.tensor_scalar(out=mprevf[:], in0=cst[:, C_MPREV:C_MPREV + 512], scalar1=flag, scalar2=None, op0=ALU.mult),
                  r=["cst", "pp"], w=["mprevf"])
            tk.op("dve", lambda e: e.tensor_scalar(out=mnextf[:], in0=cst[:, C_MNEXT:C_MNEXT + 512], scalar1=flag, scalar2=None, op0=ALU.mult),
                  r=["cst", "pp"], w=["mnextf"])
            tk.op("act", lambda e: e.activation(out=esink[:], in_=rows[:, R_SINK:R_SINK + 8], func=ACT.Exp), r=["rows"], w=["esink"])
            freeA = list(range(6))

            def getA(n):
                while len(freeA) < n:
                    yield
                out = []
                for _ in range(n):
                    i = freeA.pop(0)
                    out.append((pf[i], ("pf", i)))
                return out

            def relA(*bks):
                for bk in bks:
                    freeA.append(bk[1])

            NCH = 4

            def attn(j, cs_):
                ab, abk = attb[cs_], ("attb", cs_)
                for kv in range(2):
                    jjs = [jj for jj in (j - 1, j, j + 1) if 0 <= jj < NT]
                    bl = yield from getA(len(jjs))
                    pts = []
                    for n, jj in enumerate(jjs):
                        bank, bk = bl[n]
                        tk.op("pe", lambda e, jj=jj, bank=bank: e.matmul(bank[:, 0:512], lhsT=kT[kv * 64:(kv + 1) * 64, jj * 128:(jj + 1) * 128],
                                                                       rhs=qT[kv * 64:(kv + 1) * 64, j, :, :], start=True, stop=True),
                              r=["kT", "qT"], w=[bk])
                    yield
                    for n, jj in enumerate(jjs):
                        bank, bk = bl[n]
                        p_, pk_ = pT[cs_ * 6 + kv * 3 + n], ("pT", cs_ * 6 + kv * 3 + n)
                        tk.op("act", lambda e, p_=p_, bank=bank: e.activation(out=p_[:], in_=bank[:, 0:512], func=ACT.Exp, scale=0.125), r=[bk], w=[pk_])
                        pts.append((jj, p_, pk_))
                    relA(*[b[1] for b in bl])
                    yield
                    for (jj, p_, pk_) in pts:
                        if jj != j:
                            cross = (jj // SEG) != (j // SEG)
                            if jj < j:
                                m_, mk_ = (mprevf, "mprevf") if cross else (mprev, "mprev")
                            else:
                                m_, mk_ = (mnextf, "mnextf") if cross else (mnext, "mnext")
                            tk.op("pool" if jj < j else "dve", lambda e, p_=p_, m_=m_: e.tensor_tensor(out=p_[:], in0=p_[:], in1=m_[:], op=ALU.mult), r=[pk_, mk_], w=[pk_])
                    yield
                    ((bo, bok),) = yield from getA(1)
                    for i in range(4):
                        for n, (jj, p_, pk_) in enumerate(pts):
                            tk.op("pe", lambda e, i=i, jj=jj, p_=p_, n=n: e.matmul(bo[:, i * 65:(i + 1) * 65], lhsT=p_[:, i * 128:(i + 1) * 128],
                                                                                 rhs=vaug[:, jj, kv, :], start=(n == 0), stop=(n == len(pts) - 1)),
                                  r=[pk_, "vaug"], w=[bok])
                    yield
                    bo3 = bo[:, 0:260].rearrange("p (i c) -> p i c", i=4)
                    dn_ = den[:, cs_, kv * 4:(kv + 1) * 4]
                    dk_ = ("den", cs_)
                    tk.op("dve", lambda e: e.tensor_tensor(out=dn_, in0=bo3[:, :, 64], in1=esink[:, kv * 4:(kv + 1) * 4], op=ALU.add),
                          r=[bok, "esink"], w=[dk_])
                    tk.op("dve", lambda e: e.reciprocal(out=dn_, in_=dn_), r=[dk_], w=[dk_])
                    tk.op("dve", lambda e: e.tensor_tensor(out=ab[:, kv * 256:(kv + 1) * 256].rearrange("p (i d) -> p i d", i=4),
                                                           in0=bo3[:, :, 0:64], in1=bc(dn_, [[1, 4], [0, 64]]), op=ALU.mult),
                          r=[bok, dk_], w=[abk])
                    relA(bok)
                    yield
                pbt, pk = nb()
                for kc in range(4):
                    tk.op("pe", lambda e, kc=kc: e.transpose(pbt[:, kc * 128:(kc + 1) * 128], ab[:, kc * 128:(kc + 1) * 128], identb[:]),
                          r=[abk, "identb"], w=[pk])
                at_, atk = attTs[cs_], ("attTs", cs_)
                tk.op("act", lambda e: e.activation(out=at_[:], in_=pbt[:, 0:512], func=ACT.Copy), r=[pk], w=[atk])
                tk.dma("pool", mixT_d[0:4, :, j * 128:(j + 1) * 128].rearrange("k p t -> p k t"), at_[:].rearrange("p (k t) -> p k t", k=4),
                       r=[atk], w=[("mixT_d", "a", j)], stream="attT%d" % cs_)
                yield

            nxt_j = 0
            active = []
            while nxt_j < NT or active:
                while nxt_j < NT and len(active) < NCH and all(a[0] % NCH != nxt_j % NCH for a in active):
                    active.append([nxt_j, attn(nxt_j, nxt_j % NCH)])
                    nxt_j += 1
                still = []
                for a in active:
                    try:
                        next(a[1])
                        still.append(a)
                    except StopIteration:
                        pass
                active = still

        chk("A2")
        tk.barrier()
        NG = NT * 8
        with ExitStack() as pg:
            negA = sb(pg, "negA", [128, 8], F32)
            tmpg = sb(pg, "tmpg", [128, NT, 8], F32)
            tk.op("act", lambda e: e.activation(out=beta[:], in_=gates[:, :, 0:8], func=ACT.Exp, scale=-1.0), r=["gates"], w=["beta"])
            tk.op("dve", lambda e: e.tensor_scalar(out=beta[:], in0=beta[:], scalar1=1.0, scalar2=None, op0=ALU.add), r=["beta"], w=["beta"])
            tk.op("dve", lambda e: e.reciprocal(out=beta[:], in_=beta[:]), r=["beta"], w=["beta"])
            tk.op("act", lambda e: e.activation(out=negA[:], in_=rows[:, R_ALOG:R_ALOG + 8], func=ACT.Exp), r=["rows"], w=["negA"])
            tk.op("dve", lambda e: e.tensor_scalar(out=negA[:], in0=negA[:], scalar1=-1.0, scalar2=None, op0=ALU.mult), r=["negA"], w=["negA"])
            tk.op("dve", lambda e: e.tensor_tensor(out=tmpg[:], in0=gates[:, :, 8:16], in1=bc(rows[:, R_DT:R_DT + 8], [[0, NT], [1, 8]]), op=ALU.add),
                  r=["gates", "rows"], w=["tmpg"])
            tk.op("act", lambda e: e.activation(out=tmpg[:], in_=tmpg[:], func=ACT.Exp), r=["tmpg"], w=["tmpg"])
            tk.op("act", lambda e: e.activation(out=tmpg[:], in_=tmpg[:], func=ACT.Ln, bias=1.0), r=["tmpg"], w=["tmpg"])
            tk.op("dve", lambda e: e.tensor_tensor(out=gg[:], in0=tmpg[:], in1=bc(negA[:, 0:8], [[0, NT], [1, 8]]), op=ALU.mult),
                  r=["tmpg", "negA"], w=["gg"])
            ggf = gg[:, :, :].rearrange("p t h -> p (t h)")
            for c0 in range(0, NG, 512):
                c1 = min(NG, c0 + 512)
                tsl = slice(c0 // 8, c1 // 8)
                for d_, cm in ((0, C_MIF), (1, C_MIB)):
                    bank, bk = nf()
                    tk.op("pe", lambda e, bank=bank, cm=cm: e.matmul(bank[:, 0:c1 - c0], lhsT=cst[:, cm:cm + 128], rhs=ggf[:, c0:c1], start=True, stop=True),
                          r=["cst", "gg"], w=[bk])
                    tk.op("dve", lambda e, bank=bank, d_=d_: e.tensor_copy(out=gc[:, tsl, d_ * 4:(d_ + 1) * 4],
                                                                          in_=bank[:, 0:c1 - c0].rearrange("p (t h) -> p t h", h=8)[:, :, d_ * 4:(d_ + 1) * 4]),
                          r=[bk], w=["gc"])
                for jx, cm in ((0, C_I0), (1, C_I1)):
                    bank, bk = nf()
                    tk.op("pe", lambda e, bank=bank, cm=cm: e.matmul(bank[:, 0:c1 - c0], lhsT=cst[:, cm:cm + 128], rhs=ggf[:, c0:c1], start=True, stop=True),
                          r=["cst", "gg"], w=[bk])
                    ps_ = slice(jx * 64, (jx + 1) * 64)
                    tk.op("dve", lambda e, bank=bank, ps_=ps_: e.tensor_tensor(out=kds[ps_, tsl, :], in0=bank[ps_, 0:c1 - c0].rearrange("p (t h) -> p t h", h=8),
                                                                              in1=gc[ps_, tsl, :], op=ALU.subtract),
                          r=[bk, "gc"], w=["kds"])
                    tk.op("act", lambda e, bank=bank, jx=jx: e.activation(out=glb[:, tsl, jx, :], in_=bank[:, 0:c1 - c0].rearrange("p (t h) -> p t h", h=8), func=ACT.Exp),
                          r=[bk], w=["glb"])
            tk.op("act", lambda e: e.activation(out=kds[:], in_=kds[:], func=ACT.Exp), r=["kds"], w=["kds"])
            tk.op("act", lambda e: e.activation(out=egc[:], in_=gc[:], func=ACT.Exp), r=["gc"], w=["egc"])
            tk.op("dve", lambda e: e.tensor_tensor(out=bg[:], in0=egc[:], in1=beta[:], op=ALU.mult), r=["egc", "beta"], w=["bg"])

    chk("G")
    tk.barrier()
    pq.close()
    with ExitStack() as pbk:
        diag = sb(pbk, "diag", [128, 15, 128], BF16)
        pad = [sb(pbk, "pad%d" % i, [128, 2 * PADW], BF16) for i in range(3)]
        kq = sb(pbk, "kq", [128, NT, 256], BF16)
        dnTb = sb(pbk, "dnTb", [128, T], BF16)
        ktok = sb(pbk, "ktok", [128, NT, 128], BF16)
        vtok = sb(pbk, "vtok", [128, NT, 128], BF16)
        ysil = [sb(pbk, "ysil%d" % i, [128, 512], F32) for i in range(4)]
        eb = [sb(pbk, "eb%d" % i, [128, 512], F32) for i in range(4)]
        sqb = [sb(pbk, "sqb%d" % i, [128, 512], BF16) for i in range(4)]
        o_dir = [sb(pbk, "o_dir%d" % i, [128, NT, 128], F32) for i in range(2)]
        Sf = [sb(pbk, "Sf%d" % i, [128, 128], F32) for i in range(2)]
        Sb = [[sb(pbk, "Sb%d_%d" % (i, j), [128, 128], BF16) for j in range(2)] for i in range(2)]
        NS = 4
        NPRE = 6
        Mg = [[sb(pbk, "Mg%d_%d" % (d_, s_), [128, 128], F32) for s_ in range(NS)] for d_ in range(2)]
        DDE = [[sb(pbk, "DDE%d_%d" % (d_, s_), [128, 384], F32) for s_ in range(NS)] for d_ in range(2)]
        XY = [[[sb(pbk, "XY%d_%d_%d" % (d_, s_, v_), [128, 384], BF16) for v_ in range(2)] for s_ in range(NS)] for d_ in range(2)]
        TTb = [[sb(pbk, "TT%d_%d" % (d_, s_), [128, 128], BF16) for s_ in range(NS)] for d_ in range(2)]
        UW = [[sb(pbk, "UW%d_%d" % (d_, s_), [128, 256], BF16) for s_ in range(NS)] for d_ in range(2)]
        aT = [[sb(pbk, "aT%d_%d" % (d_, s_), [128, 128], BF16) for s_ in range(NS)] for d_ in range(2)]
        qgT = [[sb(pbk, "qgT%d_%d" % (d_, s_), [128, 128], BF16) for s_ in range(NS)] for d_ in range(2)]
        kdb = [[sb(pbk, "kd%d_%d" % (d_, s_), [128, 128], BF16) for s_ in range(NS)] for d_ in range(2)]
        kbg = [[sb(pbk, "kbg%d_%d" % (d_, s_), [128, 128], BF16) for s_ in range(NS)] for d_ in range(2)]
        vbb = [[sb(pbk, "vb%d_%d" % (d_, s_), [128, 128], BF16) for s_ in range(NS)] for d_ in range(2)]
        vnew = [sb(pbk, "vnew%d" % d_, [128, 128], BF16) for d_ in range(2)]
        msn = sb(pbk, "msn", [128, NT], F32)
        szh, dnb, dnT = vtok, ktok, dnTb

        for h in range(4):
            for j in range(15):
                ct = (j // 5) * 4 + h
                tk.op("dve", lambda e, j=j, ct=ct: e.tensor_scalar(out=diag[:, j, :], in0=ident, scalar1=pp[:, P_CONV + ct * 5 + (j % 5):P_CONV + ct * 5 + (j % 5) + 1],
                                                                  scalar2=None, op0=ALU.mult), r=["cst", "pp"], w=["diag"])
            for w3 in range(3):
                ct = w3 * 4 + h
                pd, pdk = pad[w3], ("pad", w3)
                tk.op("pool", lambda e, pd=pd: e.memset(pd[:], 0.0), w=[pdk])
                for sg in range(2):
                    tk.dma("sp", pd[:, sg * PADW + 2:sg * PADW + 2 + SEGT], dqkv_d[ct, :, sg * SEGT:(sg + 1) * SEGT], r=[("dqkv_d", ct, m_) for m_ in range(NMT)], w=[pdk], stream="pad%d" % w3)
                tk.op("dve", lambda e, pd=pd: e.tensor_scalar(out=pd[:, SEGT + 2:SEGT + 4], in0=pd[:, PADW + 2:PADW + 4], scalar1=flag, scalar2=None, op0=ALU.mult),
                      r=[pdk, "pp"], w=[pdk])
                tk.op("dve", lambda e, pd=pd: e.tensor_scalar(out=pd[:, PADW:PADW + 2], in0=pd[:, SEGT:SEGT + 2], scalar1=flag, scalar2=None, op0=ALU.mult),
                      r=[pdk, "pp"], w=[pdk])
            free_banks = list(range(6))

            def get(n):
                while len(free_banks) < n:
                    yield
                out = []
                for _ in range(n):
                    i = free_banks.pop(0)
                    out.append((pf[i], ("pf", i)))
                return out

            def rel(*bks):
                for bk in bks:
                    free_banks.append(bk[1])

            def convblk(sg, b, w3, i2):
                tok0 = sg * SEGT + b * 512
                pd, pdk = pad[w3], ("pad", w3)
                e_, ek = eb[i2], ("eb", i2)
                y_, yk = ysil[i2], ("ysil", i2)
                s_, sk = sqb[i2], ("sqb", i2)
                ((bank, bk),) = yield from get(1)
                for j in range(5):
                    off = sg * PADW + b * 512 + j
                    tk.op("pe", lambda e, j=j, off=off: e.matmul(bank[:, 0:512], lhsT=diag[:, w3 * 5 + j, :], rhs=pd[:, off:off + 512],
                                                                  start=(j == 0), stop=(j == 4)), r=["diag", pdk], w=[bk])
                yield
                tk.op("act", lambda e: e.activation(out=e_[:], in_=bank[:, 0:512], func=ACT.Exp, scale=-1.0), r=[bk], w=[ek])
                yield
                tk.op("act", lambda e: e.activation(out=e_[:], in_=e_[:], func=ACT.Ln, bias=1.0), r=[ek], w=[ek])
                tk.op("act", lambda e: e.activation(out=e_[:], in_=e_[:], func=ACT.Exp, scale=-1.0), r=[ek], w=[ek])
                yield
                tk.op("dve", lambda e: e.tensor_tensor(out=y_[:], in0=bank[:, 0:512], in1=e_[:], op=ALU.mult), r=[bk, ek], w=[yk])
                rel(bk)
                yield
                if w3 < 2:
                    tk.op("act", lambda e: e.activation(out=s_[:], in_=y_[:], func=ACT.Square), r=[yk], w=[sk])
                    yield
                    ((bank2, bk2),) = yield from get(1)
                    tk.op("pe", lambda e: e.matmul(bank2[:, 0:512], lhsT=onesb[:], rhs=s_[:], start=True, stop=True), r=["onesb", sk], w=[bk2])
                    yield
                    tk.op("act", lambda e: e.activation(out=e_[:], in_=bank2[:, 0:512], func=ACT.Ln, bias=EPS), r=[bk2], w=[ek])
                    rel(bk2)
                    tk.op("act", lambda e: e.activation(out=e_[:], in_=e_[:], func=ACT.Exp, scale=-0.5), r=[ek], w=[ek])
                    yield
                    if w3 == 0:
                        tk.op("dve", lambda e: e.scalar_tensor_tensor(out=kq[:, tok0 // 128:tok0 // 128 + 4, 128:256], in0=y_[:].rearrange("p (a t) -> p a t", a=4),
                                                                      scalar=128.0 ** -0.5, in1=e_[:].rearrange("p (a t) -> p a t", a=4),
                                                                      op0=ALU.mult, op1=ALU.mult), r=[yk, ek], w=["qTh"])
                        yield
                    else:
                        tk.op("dve", lambda e: e.tensor_tensor(out=kq[:, tok0 // 128:tok0 // 128 + 4, 0:128], in0=y_[:].rearrange("p (a t) -> p a t", a=4),
                                                               in1=e_[:].rearrange("p (a t) -> p a t", a=4), op=ALU.mult), r=[yk, ek], w=["kTh"])
                        yield
                        pbt, pk = nb()
                        for q4 in range(4):
                            tk.op("pe", lambda e, q4=q4: e.transpose(pbt[:, q4 * 128:(q4 + 1) * 128], kq[:, tok0 // 128 + q4, 0:128], identb[:]),
                                  r=["kTh", "identb"], w=[pk])
                        tk.op("act", lambda e: e.activation(out=ktok[:, tok0 // 128:tok0 // 128 + 4, :], in_=pbt[:, 0:512].rearrange("p (a d) -> p a d", a=4), func=ACT.Copy),
                              r=[pk], w=["ktok"])
                        yield
                else:
                    tk.op("dve", lambda e: e.tensor_copy(out=s_[:], in_=y_[:]), r=[yk], w=[sk])
                    yield
                    pbt, pk = nb()
                    for q4 in range(4):
                        tk.op("pe", lambda e, q4=q4: e.transpose(pbt[:, q4 * 128:(q4 + 1) * 128], s_[:, q4 * 128:(q4 + 1) * 128], identb[:]),
                              r=[sk, "identb"], w=[pk])
                    tk.op("act", lambda e: e.activation(out=vtok[:, tok0 // 128:tok0 // 128 + 4, :], in_=pbt[:, 0:512].rearrange("p (a d) -> p a d", a=4), func=ACT.Copy),
                          r=[pk], w=["vtok"])
                    yield

            blocks = [(sg, b, w3) for sg in range(2) for b in range(SEGT // 512) for w3 in range(3)]
            NBUF = 4
            nxt_blk = 0
            active = []
            while nxt_blk < len(blocks) or active:
                while nxt_blk < len(blocks) and len(active) < NBUF and all(a[0] % NBUF != nxt_blk % NBUF for a in active):
                    sg, b, w3 = blocks[nxt_blk]
                    active.append([nxt_blk, convblk(sg, b, w3, nxt_blk % NBUF)])
                    nxt_blk += 1
                still = []
                for a in active:
                    try:
                        next(a[1])
                        still.append(a)
                    except StopIteration:
                        pass
                active = still
            chk("B0")
            CM = [(C_MIF, C_MSF, C_NSF, C_MTF), (C_MIB, C_MSB, C_NSB, C_MTB)]

            def pre(d_, t, s_):
                cmi, cms, cns, cmt = CM[d_]
                hd = d_ * 4 + h
                K = lambda nm: (nm, d_, s_)
                gcol = gg[:, t, hd:hd + 1]
                mg = Mg[d_][s_]
                tk.op("dve", lambda e: e.tensor_scalar(out=mg[:], in0=cst[:, cmi:cmi + 128], scalar1=gcol, scalar2=None, op0=ALU.mult),
                      r=["cst", "gg"], w=[K("Mg")])
                tk.op("dve", lambda e: e.tensor_scalar(out=kdb[d_][s_][:], in0=ktok[:, t, :], scalar1=kds[:, t, hd:hd + 1], scalar2=None, op0=ALU.mult),
                      r=["ktok", "kds"], w=[K("kd")])
                tk.op("dve", lambda e: e.tensor_scalar(out=kbg[d_][s_][:], in0=ktok[:, t, :], scalar1=bg[:, t, hd:hd + 1], scalar2=None, op0=ALU.mult),
                      r=["ktok", "bg"], w=[K("kbg")])
                tk.op("dve", lambda e: e.tensor_scalar(out=vbb[d_][s_][:], in0=vtok[:, t, :], scalar1=beta[:, t, hd:hd + 1], scalar2=None, op0=ALU.mult),
                      r=["vtok", "beta"], w=[K("vb")])
                yield
                ksl = kq[:, t, 0:128]
                qsl = kq[:, t, 128:256]
                dde = DDE[d_][s_]
                Dm_, DTm_, Eg_ = dde[:, 0:128], dde[:, 128:256], dde[:, 256:384]
                ((bank3, bk3),) = yield from get(1)
                for n_, (lh, rh) in enumerate(((mg[:], cst[:, cms:cms + 128]), (cst[:, cms:cms + 128], mg[:]), (cst[:, C_ONE:C_ONE + 128], mg[:]))):
                    tk.op("pe", lambda e, n_=n_, lh=lh, rh=rh: e.matmul(bank3[:, n_ * 128:(n_ + 1) * 128], lhsT=lh, rhs=rh, start=True, stop=True), r=["cst", K("Mg")], w=[bk3])
                yield
                tk.op("act", lambda e: e.activation(out=dde[:, 0:384], in_=bank3[:, 0:384], func=ACT.Exp), r=[bk3], w=[K("Dm"), K("DTm"), K("Eg")])
                rel(bk3)
                yield
                tk.op("pool", lambda e: e.tensor_tensor(out=qgT[d_][s_][:], in0=qsl, in1=Eg_, op=ALU.mult), r=["qTh", K("Eg")], w=[K("qgT")])
                ((bankG, bkG),) = yield from get(1)
                tk.op("pe", lambda e: e.matmul(bankG[:, 0:256], lhsT=ksl, rhs=kq[:, t, :], start=True, stop=True), r=["kTh", "qTh"], w=[bkG])
                yield
                xy0 = XY[d_][s_][0]
                X0 = xy0[:, 256:384]
                tk.op("dve", lambda e: e.tensor_tensor(out=dde[:, 0:256], in0=bankG[:, 0:256], in1=dde[:, 0:256], op=ALU.mult), r=[bkG, K("Dm"), K("DTm")], w=[K("Dm"), K("DTm")])
                rel(bkG)
                tk.op("dve", lambda e: e.scalar_tensor_tensor(out=X0, in0=Dm_, scalar=beta[:, t, hd:hd + 1], in1=cst[:, cns:cns + 128], op0=ALU.mult, op1=ALU.mult),
                      r=[K("Dm"), "beta", "cst"], w=[K("XY0")])
                tk.op("pool", lambda e: e.tensor_tensor(out=aT[d_][s_][:], in0=DTm_, in1=cst[:, cmt:cmt + 128], op=ALU.mult), r=[K("DTm"), "cst"], w=[K("aT")])
                yield
                bankT, bkT = nb()
                tk.op("pe", lambda e: e.transpose(bankT[:, 0:128], X0, identb[:]), r=[K("XY0"), "identb"], w=[bkT])
                tk.op("act", lambda e: e.activation(out=xy0[:, 0:128], in_=bankT[:, 0:128], func=ACT.Copy), r=[bkT], w=[K("XY0")])
                tk.op("dve", lambda e: e.tensor_tensor(out=xy0[:, 128:256], in0=bankT[:, 0:128], in1=ident, op=ALU.add), r=[bkT, "cst"], w=[K("XY0")])
                yield
                cur = 0
                for j in range(6):
                    xc, xck = XY[d_][s_][cur], K("XY%d" % cur)
                    nxt = 1 - cur
                    xn, xnk = XY[d_][s_][nxt], K("XY%d" % nxt)
                    Xc, Yc, Rc = xc[:, 256:384], xc[:, 0:128], xc[:, 128:256]
                    ((bank, bk),) = yield from get(1)
                    if j == 5:
                        tk.op("pe", lambda e: e.matmul(bank[:, 0:128], lhsT=Xc, rhs=Rc, start=True, stop=True), r=[xck], w=[bk])
                        yield
                        tk.op("dve", lambda e: e.tensor_tensor(out=TTb[d_][s_][:], in0=bank[:, 0:128], in1=Rc, op=ALU.add), r=[bk, xck], w=[K("TT")])
                        rel(bk)
                        yield
                        break
                    lo = 0 if j < 4 else 128
                    hi = 128 if j == 0 else 256
                    tk.op("pe", lambda e: e.matmul(bank[:, lo:hi], lhsT=Xc, rhs=xc[:, lo:hi], start=True, stop=True), r=[xck], w=[bk])
                    tk.op("pe", lambda e: e.matmul(bank[:, 256:384], lhsT=Yc, rhs=Xc, start=True, stop=True), r=[xck], w=[bk])
                    yield
                    if j < 4:
                        tk.op("act", lambda e: e.activation(out=bc(xn[:, 0:384], [[256, 2], [1, 128]]), in_=bc(bank[:, 0:384], [[256, 2], [1, 128]]), func=ACT.Copy),
                              r=[bk], w=[xnk])
                    else:
                        tk.op("act", lambda e: e.activation(out=xn[:, 256:384], in_=bank[:, 256:384], func=ACT.Copy), r=[bk], w=[xnk])
                    if j == 0:
                        tk.op("pool", lambda e: e.tensor_copy(out=xn[:, 128:256], in_=Rc), r=[xck], w=[xnk])
                    else:
                        tk.op("dve", lambda e: e.tensor_tensor(out=xn[:, 128:256], in0=bank[:, 128:256], in1=Rc, op=ALU.add), r=[bk, xck], w=[xnk])
                    rel(bk)
                    yield
                    cur = nxt
                ((bankU, bkU),) = yield from get(1)
                tk.op("pe", lambda e: e.matmul(bankU[:, 0:128], lhsT=TTb[d_][s_][:], rhs=vbb[d_][s_][:], start=True, stop=True), r=[K("TT"), K("vb")], w=[bkU])
                tk.op("pe", lambda e: e.matmul(bankU[:, 128:256], lhsT=kbg[d_][s_][:], rhs=TTb[d_][s_][:], start=True, stop=True), r=[K("TT"), K("kbg")], w=[bkU])
                yield
                tk.op("act", lambda e: e.activation(out=UW[d_][s_][:], in_=bankU[:, 0:256], func=ACT.Copy), r=[bkU], w=[K("u"), K("wTn")])
                rel(bkU)
                yield

            sbi = [0, 0]

            def scan(d_, t, s_):
                hd = d_ * 4 + h
                K = lambda nm: (nm, d_, s_)
                if (d_ == 0 and t == SEG) or (d_ == 1 and t == SEG - 1):
                    tk.op("dve", lambda e: e.tensor_scalar(out=Sf[d_][:], in0=Sf[d_][:], scalar1=flag, scalar2=None, op0=ALU.mult), r=[("Sf", d_), "pp"], w=[("Sf", d_)])
                    cb_ = sbi[d_]
                    tk.op("dve", lambda e: e.tensor_scalar(out=Sb[d_][cb_][:], in0=Sb[d_][cb_][:], scalar1=flag, scalar2=None, op0=ALU.mult),
                          r=[("Sb", d_, cb_), "pp"], w=[("Sb", d_, cb_)])
                    yield
                for ci in range(2):
                    ch = ci if d_ == 0 else 1 - ci
                    ps_ = slice(ch * 64, (ch + 1) * 64)
                    Sbc, Sbk = Sb[d_][sbi[d_]], ("Sb", d_, sbi[d_])
                    (bw, bwk), (bo_, bok_), (bs, bsk) = yield from get(3)
                    tk.op("pe", lambda e, bw=bw, Sbc=Sbc: e.matmul(bw[:, 0:128], lhsT=UW[d_][s_][:, 128:256], rhs=Sbc[:], start=True, stop=True), r=[K("wTn"), Sbk], w=[bwk])
                    tk.op("pe", lambda e, bo_=bo_, Sbc=Sbc: e.matmul(bo_[:, 0:128], lhsT=qgT[d_][s_][:], rhs=Sbc[:], start=True, stop=False), r=[K("qgT"), Sbk], w=[bok_])
                    yield
                    vn, vnk = vnew[d_], ("vnew", d_)
                    tk.op("dve", lambda e, bw=bw, vn=vn, ps_=ps_: e.tensor_tensor(out=vn[ps_, :], in0=UW[d_][s_][ps_, 0:128], in1=bw[ps_, 0:128], op=ALU.subtract), r=[bwk, K("u")], w=[vnk])
                    rel(bwk)
                    yield
                    tk.op("pe", lambda e, bo_=bo_, vn=vn, ps_=ps_: e.matmul(bo_[:, 0:128], lhsT=aT[d_][s_][ps_, :], rhs=vn[ps_, :], start=False, stop=True), r=[K("aT"), vnk], w=[bok_])
                    tk.op("pe", lambda e, bs=bs, vn=vn, ps_=ps_: e.matmul(bs[:, 0:128], lhsT=kdb[d_][s_][ps_, :], rhs=vn[ps_, :], start=True, stop=True), r=[K("kd"), vnk], w=[bsk])
                    yield
                    nsb = 1 - sbi[d_]
                    Sbn, Sbnk = Sb[d_][nsb], ("Sb", d_, nsb)
                    glc = glb[:, t, ch, hd:hd + 1]
                    tk.op("dve", lambda e, bs=bs, Sbn=Sbn, glc=glc: e.scalar_tensor_tensor(out=Sbn[:], in0=Sf[d_][:], scalar=glc, in1=bs[:, 0:128], op0=ALU.mult, op1=ALU.add),
                          r=[("Sf", d_), "glb", bsk], w=[Sbnk])
                    tk.op("dve", lambda e, bs=bs, glc=glc: e.scalar_tensor_tensor(out=Sf[d_][:], in0=Sf[d_][:], scalar=glc, in1=bs[:, 0:128], op0=ALU.mult, op1=ALU.add),
                          r=[("Sf", d_), "glb", bsk], w=[("Sf", d_)])
                    sbi[d_] = nsb
                    tk.op("act", lambda e, bo_=bo_, ps_=ps_: e.activation(out=o_dir[d_][ps_, t, :], in_=bo_[ps_, 0:128], func=ACT.Copy), r=[bok_], w=[("o_dir", d_)])
                    rel(bok_, bsk)
                    yield

            for d_ in range(2):
                tk.op("pool", lambda e, d_=d_: e.memset(Sf[d_][:], 0.0), w=[("Sf", d_)])
                tk.op("pool", lambda e, d_=d_: e.memset(Sb[d_][0][:], 0.0), w=[("Sb", d_, 0)])
                sbi[d_] = 0
            tile_of = lambda d_, i: i if d_ == 0 else NT - 1 - i
            pre_next = [0, 0]
            pre_done = [0, 0]
            scan_next = [0, 0]
            scan_done = [0, 0]
            active = []
            while scan_done[0] < NT or scan_done[1] < NT:
                for d_ in range(2):
                    while pre_next[d_] < NT and pre_next[d_] - scan_done[d_] < NS and sum(1 for a in active if a[0] == "pre") < NPRE:
                        i = pre_next[d_]
                        active.append(["pre", d_, i, pre(d_, tile_of(d_, i), i % NS)])
                        pre_next[d_] += 1
                    if scan_next[d_] < NT and scan_next[d_] == scan_done[d_] and pre_done[d_] > scan_next[d_]:
                        i = scan_next[d_]
                        active.append(["scan", d_, i, scan(d_, tile_of(d_, i), i % NS)])
                        scan_next[d_] += 1
                still = []
                for a in active:
                    try:
                        next(a[3])
                        still.append(a)
                    except StopIteration:
                        if a[0] == "pre":
                            a[0] = "pre_fin"
                            still.append(a)
                        else:
                            scan_done[a[1]] += 1
                for d_ in range(2):
                    while True:
                        fin = [a for a in still if a[0] == "pre_fin" and a[1] == d_ and a[2] == pre_done[d_]]
                        if not fin:
                            break
                        still.remove(fin[0])
                        pre_done[d_] += 1
                active = still

            if DEBUG and h == 3:
                dq = nc.dram_tensor("dbg_q", [128, T], BF16, kind="ExternalOutput").ap()
                dk = nc.dram_tensor("dbg_k", [128, T], BF16, kind="ExternalOutput").ap()
                do0 = nc.dram_tensor("dbg_o0", [128, NT * 128], F32, kind="ExternalOutput").ap()
                do1 = nc.dram_tensor("dbg_o1", [128, NT * 128], F32, kind="ExternalOutput").ap()
                dgg = nc.dram_tensor("dbg_gg", [128, NT * 8], F32, kind="ExternalOutput").ap()
                dbeta = nc.dram_tensor("dbg_beta", [128, NT * 8], F32, kind="ExternalOutput").ap()
                dkds = nc.dram_tensor("dbg_kds", [128, NT * 8], F32, kind="ExternalOutput").ap()
                dvt = nc.dram_tensor("dbg_vtok", [128, NT * 128], BF16, kind="ExternalOutput").ap()
                tk.dma("pool", do0[:, :], o_dir[0][:].rearrange("p t d -> p (t d)"), r=[("o_dir", 0)], stream="dbg")
                tk.dma("pool", do1[:, :], o_dir[1][:].rearrange("p t d -> p (t d)"), r=[("o_dir", 1)], stream="dbg")
                tk.dma("pool", dgg[:, :], gg[:].rearrange("p t d -> p (t d)"), r=["gg"], stream="dbg")
                tk.dma("pool", dbeta[:, :], beta[:].rearrange("p t d -> p (t d)"), r=["beta"], stream="dbg")
                tk.dma("pool", dkds[:, :], kds[:].rearrange("p t d -> p (t d)"), r=["kds"], stream="dbg")
                tk.dma("pool", dvt[:, :], vtok[:].rearrange("p t d -> p (t d)"), r=["vtok"], stream="dbg")
            chk("B1")
            tk.dma("sp", szh[:], sz_d[:, h * 128:(h + 1) * 128].rearrange("(t p) d -> p t d", p=128), r=[("sz_d", t_) for t_ in range(NT)], w=["vtok"], stream="szh")
            tmpo = o_dir[0]
            tk.op("dve", lambda e: e.tensor_tensor(out=tmpo[:], in0=o_dir[0][:], in1=o_dir[1][:], op=ALU.add), r=[("o_dir", 0), ("o_dir", 1)], w=["tmpo", ("o_dir", 0)])
            tk.op("pool", lambda e: e.tensor_tensor(out=o_dir[1][:], in0=tmpo[:], in1=tmpo[:], op=ALU.mult), r=["tmpo"], w=[("o_dir", 1)])
            tk.op("dve", lambda e: e.tensor_reduce(out=msn[:], in_=o_dir[1][:], axis=AX.X, op=ALU.add), r=[("o_dir", 1)], w=["msn"])
            tk.op("act", lambda e: e.activation(out=msn[:], in_=msn[:], func=ACT.Ln, bias=EPS, scale=1.0 / 128.0), r=["msn"], w=["msn"])
            tk.op("act", lambda e: e.activation(out=msn[:], in_=msn[:], func=ACT.Exp, scale=-0.5), r=["msn"], w=["msn"])
            tk.op("dve", lambda e: e.tensor_tensor(out=tmpo[:], in0=tmpo[:], in1=bc(msn[:, 0:NT], [[1, NT], [0, 128]]), op=ALU.mult), r=["tmpo", "msn"], w=["tmpo"])
            tk.op("dve", lambda e: e.tensor_tensor(out=tmpo[:], in0=tmpo[:], in1=bc(rows[:, R_DNW:R_DNW + 128], [[0, NT], [1, 128]]), op=ALU.mult), r=["tmpo", "rows"], w=["tmpo"])
            tk.op("dve", lambda e: e.tensor_tensor(out=dnb[:], in0=tmpo[:], in1=szh[:], op=ALU.mult), r=["tmpo", "vtok"], w=["ktok", ("o_dir", 0)])
            for t0 in range(0, NT, 8):
                pbt, pk = nb()
                for q8 in range(8):
                    tk.op("pe", lambda e, q8=q8, pbt=pbt, t0=t0: e.transpose(pbt[:, q8 * 128:(q8 + 1) * 128], dnb[:, t0 + q8, :], identb[:]), r=["ktok", "identb"], w=[pk])
                tk.op("act", lambda e, pbt=pbt, t0=t0: e.activation(out=dnT[:, t0 * 128:(t0 + 8) * 128], in_=pbt[:, :], func=ACT.Copy), r=[pk], w=["dnTb"])
            tk.dma("pool", mixT_d[4 + h, :, :], dnT[:], r=["dnTb"], w=[("mixT_d", "d", h)], stream="dnT")

    chk("B")
    tk.barrier()
    pgs.close()
    with ExitStack() as pc:
        stg = [sb(pc, "stgC%d" % i, [128, 1024], F32) for i in range(2)]
        w_out_b = sb(pc, "w_out_b", [128, 8, 1024], BF16)
        wq_b = sb(pc, "wq_b", [128, 8, 512], BF16)
        wkv_b = sb(pc, "wkv_b", [128, 8, 1024], BF16)
        wo_b = sb(pc, "wo_b", [128, 4, 1024], BF16)
        KmT = sb(pc, "KmT", [128, 2, 4, 256], BF16)
        Vx = sb(pc, "Vx", [128, 2, 2, 4, 129], BF16)
        mixT = [sb(pc, "mixT%d" % i, [128, 8, 512], BF16) for i in range(2)]
        x1 = sb(pc, "x1", [128, 4, D], F32)
        h2T = sb(pc, "h2T", [128, 8, 512], BF16)
        qxT = sb(pc, "qxT", [128, 4, 512], BF16)
        pTx = sb(pc, "pTx", [128, 8, 512], BF16)
        x2o = [sb(pc, "x2o%d" % i, [128, D], F32) for i in range(4)]
        xr4 = [sb(pc, "xr4_%d" % i, [128, D], F32) for i in range(4)]
        tmp4 = [sb(pc, "tmp4_%d" % i, [128, D], F32) for i in range(4)]
        hb4 = [sb(pc, "hb4_%d" % i, [128, D], BF16) for i in range(4)]
        oxb4 = [sb(pc, "oxb4_%d" % i, [128, 512], BF16) for i in range(4)]
        oxT4 = [sb(pc, "oxT4_%d" % i, [128, 4, 128], BF16) for i in range(4)]
        rdx4 = sb(pc, "rdx4", [128, 4, 4], F32)
        smc4 = sb(pc, "smc4", [128, 4, 4], F32)
        gp2 = sb(pc, "gp2", [128, 2048], F32)
        tk.dma("sp", gp2[:], rows_d[:, 0:2048], w=["gp2"], stream="c_pp")
        stgC = [(stg[i], ("stgC", i), "stgC%d" % i) for i in range(2)] + [(xr4[i], ("xr4", i), "stgC%d" % (2 + i)) for i in range(4)]
        load_weight_bf16(stgC, w_out_b, "w_out_b", w_out_d, 8, 1024)
        load_weight_bf16(stgC, wq_b, "wq_b", wq_d, 8, 512, gcol=P_GXA)
        load_weight_bf16(stgC, wkv_b, "wkv_b", wkv_d, 8, 1024, gcol=P_GMEM)
        load_weight_bf16(stgC, wo_b, "wo_b", wo_d, 4, 1024)
        tk.op("pool", lambda e: e.memset(Vx[:], 1.0), w=["Vx"])
        for sg in range(2):
            for kt in range(2):
                xs, xk = xr4[kt], ("xr4", kt)
                tk.dma("sp", xs[:], mem_d[sg, kt * 128:(kt + 1) * 128, :], w=[xk], stream="xr4_%d" % kt)
                norm_transpose(xs[:], xk, hb4[0][:], ("hb4", 0), h2T[:, :, kt * 128:(kt + 1) * 128], "h2T", tmp4[0][:], ("tmp4", 0))
            for hx in range(4):
                bank, bk = nf()
                for kc in range(8):
                    tk.op("pe", lambda e, kc=kc, bank=bank, hx=hx: e.matmul(bank[:, 0:256], lhsT=wkv_b[:, kc, hx * 128:(hx + 1) * 128], rhs=h2T[:, kc, 0:256],
                                                                          start=(kc == 0), stop=(kc == 7)), r=["wkv_b", "h2T"], w=[bk])
                tk.op("act", lambda e, bank=bank, hx=hx, sg=sg: e.activation(out=KmT[:, sg, hx, :], in_=bank[:, 0:256], func=ACT.Copy), r=[bk], w=["KmT"])
            for kt in range(2):
                bank, bk = nf()
                for kc in range(8):
                    tk.op("pe", lambda e, kc=kc, bank=bank, kt=kt: e.matmul(bank[:, 0:512], lhsT=h2T[:, kc, kt * 128:(kt + 1) * 128], rhs=wkv_b[:, kc, 512:1024],
                                                                          start=(kc == 0), stop=(kc == 7)), r=["wkv_b", "h2T"], w=[bk])
                tk.op("act", lambda e, bank=bank, kt=kt, sg=sg: e.activation(out=Vx[:, sg, kt, :, 0:128], in_=bank[:, 0:512].rearrange("p (a d) -> p a d", a=4), func=ACT.Copy),
                      r=[bk], w=["Vx"])

        freeC = list(range(6))

        def getC(n):
            while len(freeC) < n:
                yield
            out = []
            for _ in range(n):
                i = freeC.pop(0)
                out.append((pf[i], ("pf", i)))
            return out

        def relC(*bks):
            for bk in bks:
                freeC.append(bk[1])

        def run_chains(gens):
            active = list(gens)
            while active:
                still = []
                for g in active:
                    try:
                        next(g)
                        still.append(g)
                    except StopIteration:
                        pass
                active = still

        def post_norm_gen(bl, grow, growkey, res_ap, reskey, out_ap, outkey, tmp, tmpkey, smt, smk, add_eng="pool"):
            (b0, k0_), (b1, k1_) = bl
            banks, bkeys = [b0, b1], [k0_, k1_]
            for hf in range(2):
                tk.op("act", lambda e, hf=hf: e.activation(out=tmp[:, 0:512], in_=banks[hf][:, 0:512], func=ACT.Square, scale=1.0 / 32.0,
                                                           accum_out=smt[:, hf:hf + 1]), r=[bkeys[hf]], w=[tmpkey, smk])
            yield
            tk.op("dve", lambda e: e.tensor_tensor(out=smt[:, 0:1], in0=smt[:, 0:1], in1=smt[:, 1:2], op=ALU.add), r=[smk], w=[smk])
            yield
            tk.op("act", lambda e: e.activation(out=smt[:, 0:1], in_=smt[:, 0:1], func=ACT.Ln, bias=EPS, scale=1.0), r=[smk], w=[smk])
            tk.op("act", lambda e: e.activation(out=smt[:, 0:1], in_=smt[:, 0:1], func=ACT.Exp, scale=-0.5), r=[smk], w=[smk])
            yield
            for hf in range(2):
                tk.op("dve", lambda e, hf=hf: e.scalar_tensor_tensor(out=tmp[:, hf * 512:(hf + 1) * 512], in0=banks[hf][:, 0:512], scalar=smt[:, 0:1],
                                                                     in1=grow[:, hf * 512:(hf + 1) * 512], op0=ALU.mult, op1=ALU.mult),
                      r=[bkeys[hf], smk, growkey], w=[tmpkey])
            relC(k0_, k1_)
            yield
            tk.op(add_eng, lambda e: e.tensor_tensor(out=out_ap, in0=tmp, in1=res_ap, op=ALU.add), r=[tmpkey, reskey], w=[outkey])
            yield

        def norm_transpose_gen(xs, xkey, hb, hbkey, hT_out, hTkey, junk, junkkey, smt, smk):
            tk.op("act", lambda e: e.activation(out=junk, in_=xs, func=ACT.Square, scale=1.0 / 32.0, accum_out=smt[:, 2:3]), r=[xkey], w=[junkkey, smk])
            yield
            tk.op("act", lambda e: e.activation(out=smt[:, 2:3], in_=smt[:, 2:3], func=ACT.Ln, bias=EPS, scale=1.0), r=[smk], w=[smk])
            tk.op("act", lambda e: e.activation(out=smt[:, 2:3], in_=smt[:, 2:3], func=ACT.Exp, scale=-0.5), r=[smk], w=[smk])
            yield
            tk.op("dve", lambda e: e.tensor_scalar(out=hb, in0=xs, scalar1=smt[:, 2:3], scalar2=None, op0=ALU.mult), r=[xkey, smk], w=[hbkey])
            yield
            pbt, pk = nb()
            for kc in range(8):
                tk.op("pe", lambda e, kc=kc: e.transpose(pbt[:, kc * 128:(kc + 1) * 128], hb[:, kc * 128:(kc + 1) * 128], identb[:]), r=[hbkey, "identb"], w=[pk])
            tk.op("act", lambda e: e.activation(out=hT_out, in_=pbt[:, :].rearrange("p (k t) -> p k t", k=8), func=ACT.Copy), r=[pk], w=[hTkey])
            yield

        for mt in range(NMT):
            sg = (mt * 4) // SEG
            mx, mxk = mixT[mt % 2], ("mixT", mt % 2)
            tk.dma("sp", mx[:], mixT_d[:, :, mt * 512:(mt + 1) * 512].rearrange("k p t -> p k t"),
                   r=[("mixT_d", "a", j_) for j_ in range(mt * 4, mt * 4 + 4)] + [("mixT_d", "d", h_) for h_ in range(4)], w=[mxk], stream="mixT%d" % (mt % 2))

            def c1(s):
                t = mt * 4 + s
                xs, xk = xr4[s], ("xr4", s)
                tk.dma("sp", xs[:], x_d[t * 128:(t + 1) * 128, :], w=[xk], stream="xr4_%d" % s)
                bl = yield from getC(2)
                for hf in range(2):
                    bank, bk = bl[hf]
                    for kc in range(8):
                        tk.op("pe", lambda e, kc=kc, bank=bank, hf=hf: e.matmul(bank[:, 0:512], lhsT=mx[:, kc, s * 128:(s + 1) * 128], rhs=w_out_b[:, kc, hf * 512:(hf + 1) * 512],
                                                                              start=(kc == 0), stop=(kc == 7)), r=[mxk, "w_out_b"], w=[bk])
                yield
                yield from post_norm_gen(bl, gp2[:, 0:1024], "gp2", xs[:], xk, x1[:, s, :], ("x1", s), tmp4[s][:], ("tmp4", s), smc4[:, s, :], ("smc4", s))
                yield from norm_transpose_gen(x1[:, s, :], ("x1", s), hb4[s][:], ("hb4", s), h2T[:, :, s * 128:(s + 1) * 128], "h2T", tmp4[s][:], ("tmp4", s),
                                              smc4[:, s, :], ("smc4", s))
            run_chains([c1(s) for s in range(4)])

            for hx in range(4):
                bank, bk = nf()
                for kc in range(8):
                    tk.op("pe", lambda e, kc=kc, bank=bank, hx=hx: e.matmul(bank[:, 0:512], lhsT=wq_b[:, kc, hx * 128:(hx + 1) * 128], rhs=h2T[:, kc, :],
                                                                          start=(kc == 0), stop=(kc == 7)), r=["wq_b", "h2T"], w=[bk])
                tk.op("act", lambda e, bank=bank, hx=hx: e.activation(out=qxT[:, hx, :], in_=bank[:, 0:512], func=ACT.Copy, scale=128.0 ** -0.5), r=[bk], w=["qxT"])
            for hx in range(4):
                for kt in range(2):
                    bank, bk = nf()
                    tk.op("pe", lambda e, bank=bank, hx=hx, kt=kt: e.matmul(bank[:, 0:512], lhsT=KmT[:, sg, hx, kt * 128:(kt + 1) * 128], rhs=qxT[:, hx, :], start=True, stop=True),
                          r=["KmT", "qxT"], w=[bk])
                    tk.op("act", lambda e, bank=bank, hx=hx, kt=kt: e.activation(out=pTx[:, hx * 2 + kt, :], in_=bank[:, 0:512], func=ACT.Exp), r=[bk], w=["pTx"])

            def c3(s):
                t = mt * 4 + s
                ox_, oxk = oxb4[s], ("oxb4", s)
                oT_, oTk = oxT4[s], ("oxT4", s)
                bl = yield from getC(2)
                for hp in range(2):
                    bo, bok = bl[hp]
                    for hh in range(2):
                        hx = hp * 2 + hh
                        for kt in range(2):
                            tk.op("pe", lambda e, bo=bo, hx=hx, hh=hh, kt=kt: e.matmul(bo[:, hh * 129:(hh + 1) * 129], lhsT=pTx[:, hx * 2 + kt, s * 128:(s + 1) * 128],
                                                                                     rhs=Vx[:, sg, kt, hx, :], start=(kt == 0), stop=(kt == 1)), r=["pTx", "Vx"], w=[bok])
                yield
                for hp in range(2):
                    bo, bok = bl[hp]
                    bo3 = bo[:, 0:258].rearrange("p (i c) -> p i c", i=2)
                    rd = rdx4[:, s, hp * 2:(hp + 1) * 2]
                    tk.op("dve", lambda e, bo3=bo3, rd=rd: e.reciprocal(out=rd, in_=bo3[:, :, 128]), r=[bok], w=[("rdx4", s)])
                    tk.op("dve", lambda e, bo3=bo3, rd=rd, hp=hp: e.tensor_tensor(out=ox_[:, hp * 256:(hp + 1) * 256].rearrange("p (i d) -> p i d", i=2), in0=bo3[:, :, 0:128],
                                                                                  in1=bc(rd, [[1, 2], [0, 128]]), op=ALU.mult), r=[bok, ("rdx4", s)], w=[oxk])
                relC(bl[0][1], bl[1][1])
                yield
                pbt, pk = nb()
                for kc in range(4):
                    tk.op("pe", lambda e, kc=kc: e.transpose(pbt[:, kc * 128:(kc + 1) * 128], ox_[:, kc * 128:(kc + 1) * 128], identb[:]), r=[oxk, "identb"], w=[pk])
                tk.op("act", lambda e: e.activation(out=oT_[:], in_=pbt[:, 0:512].rearrange("p (k t) -> p k t", k=4), func=ACT.Copy), r=[pk], w=[oTk])
                yield
                bl2 = yield from getC(2)
                for hf in range(2):
                    bank, bk = bl2[hf]
                    for kc in range(4):
                        tk.op("pe", lambda e, kc=kc, bank=bank, hf=hf: e.matmul(bank[:, 0:512], lhsT=oT_[:, kc, :], rhs=wo_b[:, kc, hf * 512:(hf + 1) * 512],
                                                                              start=(kc == 0), stop=(kc == 3)), r=[oTk, "wo_b"], w=[bk])
                yield
                xo, xok = x2o[s], ("x2o", s)
                yield from post_norm_gen(bl2, gp2[:, 1024:2048], "gp2", x1[:, s, :], ("x1", s), xo[:], xok, tmp4[s][:], ("tmp4", s), smc4[:, s, :], ("smc4", s))
                tk.dma("pool", x2_d[t * 128:(t + 1) * 128, :], xo[:], r=[xok], w=[("x2_d", t)], stream="x2o%d" % s)
            run_chains([c3(s) for s in range(4)])

    chk("Ca")
    tk.barrier()
    with ExitStack() as pm:
        w1_b = sb(pm, "w1_b", [128, 8, 4096], BF16)
        w2_b = sb(pm, "w2_b", [128, 32, 1024], BF16)
        gp1 = sb(pm, "gp1", [128, 1024], F32)
        xm = [sb(pm, "xm%d" % i, [128, D], F32) for i in range(4)]
        hbm2 = [sb(pm, "hbm%d" % i, [128, D], BF16) for i in range(2)]
        junkm = sb(pm, "junkm", [128, D], BF16)
        h3T = sb(pm, "h3T", [128, 8, 256], BF16)
        uT = sb(pm, "uT", [128, 32, 256], BF16)
        rl = [sb(pm, "rl%d" % i, [128, 256], BF16) for i in range(2)]
        tmpm = sb(pm, "tmpm", [128, D], F32)
        yo = [sb(pm, "yo%d" % i, [128, D], F32) for i in range(2)]
        stg6 = [(yo[0], ("yo", 0)), (yo[1], ("yo", 1))] + [(xm[i], ("xm", i)) for i in range(4)]
        stgn = [0]

        def next_stg():
            i = stgn[0] % 6
            stgn[0] += 1
            return stg6[i][0], stg6[i][1], "stgM%d" % i
        tk.dma("sp", gp1[:], rows_d[:, 2048:3072], w=["gp1"], stream="c_rows")
        for kc in range(8):
            g = pp[:, P_GMLP + kc:P_GMLP + kc + 1]
            for q in range(4):
                sl, skey, sname = next_stg()
                tk.dma("sp", sl[:], w1_d[kc * 128:(kc + 1) * 128, q * 1024:(q + 1) * 1024], w=[skey], stream=sname)
                if q % 2 == 0:
                    tk.op("act", lambda e, g=g, sl=sl, kc=kc, q=q: e.activation(out=w1_b[:, kc, q * 1024:(q + 1) * 1024], in_=sl[:], func=ACT.Copy, scale=g),
                          r=[skey, "pp"], w=["w1_b"])
                else:
                    tk.op("dve", lambda e, g=g, sl=sl, kc=kc, q=q: e.tensor_scalar(out=w1_b[:, kc, q * 1024:(q + 1) * 1024], in0=sl[:], scalar1=g, scalar2=None, op0=ALU.mult),
                          r=[skey, "pp"], w=["w1_b"])
        for ft in range(32):
            sl, skey, sname = next_stg()
            tk.dma("sp", sl[:], w2_d[ft * 128:(ft + 1) * 128, :], w=[skey], stream=sname)
            if ft % 2 == 0:
                tk.op("act", lambda e, sl=sl, ft=ft: e.activation(out=w2_b[:, ft, :], in_=sl[:], func=ACT.Copy), r=[skey], w=["w2_b"])
            else:
                tk.op("dve", lambda e, sl=sl, ft=ft: e.tensor_copy(out=w2_b[:, ft, :], in_=sl[:]), r=[skey], w=["w2_b"])
        chk("Wm")
        def mlp_norm(mt_):
            for s in range(2):
                t = mt_ * 2 + s
                xi = (mt_ % 2) * 2 + s
                xs, xk = xm[xi], ("xm", xi)
                tk.dma("sp", xs[:], x2_d[t * 128:(t + 1) * 128, :], r=[("x2_d", t)], w=[xk], stream="xm%d" % xi)
                norm_part(xs[:], xk, hbm2[s][:], ("hbm", s), junkm[:], "junkm")

        mlp_norm(0)
        for mt in range(NT // 2):
            for s in range(2):
                transpose_part(hbm2[s][:], ("hbm", s), h3T[:, :, s * 128:(s + 1) * 128], "h3T")
            for ft in range(32):
                bank, bk = nf()
                for kc in range(8):
                    tk.op("pe", lambda e, kc=kc, bank=bank, ft=ft: e.matmul(bank[:, 0:256], lhsT=w1_b[:, kc, ft * 128:(ft + 1) * 128], rhs=h3T[:, kc, :],
                                                                          start=(kc == 0), stop=(kc == 7)), r=["w1_b", "h3T"], w=[bk])
                r_, rk = rl[ft % 2], ("rl", ft % 2)
                tk.op("act", lambda e, bank=bank, r_=r_: e.activation(out=r_[:], in_=bank[:, 0:256], func=ACT.Relu), r=[bk], w=[rk])
                en = "dve" if ft % 2 == 0 else "pool"
                tk.op(en, lambda e, r_=r_, ft=ft: e.tensor_tensor(out=uT[:, ft, :], in0=r_[:], in1=r_[:], op=ALU.mult), r=[rk], w=["uT"])
            if mt + 1 < NT // 2:
                mlp_norm(mt + 1)
            for s in range(2):
                t = mt * 2 + s
                banks, bkeys = [], []
                for hf in range(2):
                    bank, bk = nf()
                    for ft in range(32):
                        tk.op("pe", lambda e, ft=ft, bank=bank, hf=hf: e.matmul(bank[:, 0:512], lhsT=uT[:, ft, s * 128:(s + 1) * 128], rhs=w2_b[:, ft, hf * 512:(hf + 1) * 512],
                                                                              start=(ft == 0), stop=(ft == 31)), r=["uT", "w2_b"], w=[bk])
                    banks.append(bank); bkeys.append(bk)
                yo_, yok = yo[t % 2], ("yo", t % 2)
                xi = (mt % 2) * 2 + s
                post_norm_residual(banks, bkeys, gp1[:], "gp1", xm[xi][:], ("xm", xi), yo_[:], yok, junkm[:], "junkm", tmpm[:], "tmpm")
                tk.dma("pool", y_d[t * 128:(t + 1) * 128, :], yo_[:], r=[yok], stream="yo%d" % (t % 2))

    tk.finish("pool")
    es.close()


def make_consts(NT, SEG, pos):
    c = np.zeros((128, 2816), np.float32)
    idx = np.arange(128)
    same = (idx[:, None] // 64) == (idx[None, :] // 64)
    r_, c_ = idx[:, None], idx[None, :]
    mats = [
        np.eye(128),
        same & (r_ <= c_),
        same & (r_ > c_),
        same & (r_ >= c_),
        same & (r_ < c_),
        -1.0 * (same & (c_ < r_)),
        -1.0 * (same & (c_ > r_)),
        same & (r_ <= c_),
        same & (r_ >= c_),
        np.broadcast_to(idx[:, None] < 64, (128, 128)),
        np.broadcast_to(idx[:, None] >= 64, (128, 128)),
        np.ones((128, 128)),
    ]
    prot = np.zeros((128, 128))
    for m in range(128):
        mm = m % 64
        if mm < 8:
            prot[m + 8, m] = -1.0
        elif mm < 16:
            prot[m - 8, m] = 1.0
    mats.append(prot)
    for i, m in enumerate(mats):
        c[:, i * 128:(i + 1) * 128] = np.asarray(m, np.float32)
    mprev = (r_ >= c_).astype(np.float32)
    mnext = (r_ <= c_).astype(np.float32)
    c[:, 13 * 128:13 * 128 + 512] = np.tile(mprev, (1, 4))
    c[:, 13 * 128 + 512:13 * 128 + 1024] = np.tile(mnext, (1, 4))
    T = NT * 128
    half = 8
    inv = (500000.0 ** (-(np.arange(half, dtype=np.float32) * 2.0 / 16.0))).astype(np.float32)
    ang = pos.astype(np.float32)[:, None] * inv[None, :]
    cos = np.cos(ang).astype(np.float32)
    sin = np.sin(ang).astype(np.float32)
    cs = np.zeros((128, 2 * T), np.float32)
    cs[:, :T] = 1.0
    for p in range(128):
        mm = p % 64
        if mm < 16:
            cs[p, :T] = cos[:, mm % 8]
            cs[p, T:] = sin[:, mm % 8]
    return c, cs


def make_pp(inp, flagval):
    pp = np.zeros((128, 96), np.float32)
    for off, name in ((0, "norm_pre_mix"), (8, "norm_pre_xa"), (16, "norm_pre_mlp"), (24, "mem_norm_g")):
        pp[:, off:off + 8] = np.asarray(inp[name], np.float32).reshape(8, 128).T
    cw = np.asarray(inp["dn_conv_w"], np.float32).reshape(5, 12, 128)
    pp[:, 32:92] = cw.transpose(2, 1, 0).reshape(128, 60)
    pp[:, 92] = flagval
    return pp


def make_rows(inp):
    r = np.zeros((3232,), np.float32)
    r[0:1024] = np.asarray(inp["norm_post_mix"]).reshape(-1)
    r[1024:2048] = np.asarray(inp["norm_post_xa"]).reshape(-1)
    r[2048:3072] = np.asarray(inp["norm_post_mlp"]).reshape(-1)
    r[3072:3200] = np.asarray(inp["dn_norm_w"]).reshape(-1)
    r[3200:3208] = np.asarray(inp["attn_sink"]).reshape(-1)
    r[3208:3212] = np.asarray(inp["dn_A_log_f"]).reshape(-1)
    r[3212:3216] = np.asarray(inp["dn_A_log_b"]).reshape(-1)
    r[3216:3220] = np.asarray(inp["dn_dt_bias_f"]).reshape(-1)
    r[3220:3224] = np.asarray(inp["dn_dt_bias_b"]).reshape(-1)
    return np.ascontiguousarray(np.broadcast_to(r[None, :], (128, 3232)))


def run_cores(inp, core_specs, NT, SEG, stop=None):
    nc = build(NT, SEG, stop)
    rows = make_rows(inp)
    f32 = lambda a: np.ascontiguousarray(np.asarray(a, np.float32))
    shared = {
        "w_in": f32(inp["w_in"][0]), "w_out": f32(inp["w_out"][0]), "xa_wq": f32(inp["xa_wq"][0]),
        "xa_wkv": f32(inp["xa_wkv"][0]), "xa_wo": f32(inp["xa_wo"][0]), "mlp_w1": f32(inp["mlp_w1"][0]),
        "mlp_w2": f32(inp["mlp_w2"][0]), "rows": rows,
    }
    in_maps = []
    for (x, mem, flagval, pos) in core_specs:
        cst, cs = make_consts(NT, SEG, pos)
        m = dict(shared)
        m.update({"x": f32(x), "mem": f32(mem), "pp": make_pp(inp, flagval), "cst": cst, "cossin": cs})
        in_maps.append(m)
    res = run_bass_kernel_spmd(nc, in_maps, core_ids=list(range(len(core_specs))))
    if DEBUG:
        for i, r in enumerate(res.results):
            DBG_OUT[i] = {k: np.asarray(v) for k, v in r.items()}
    return [r["y"] for r in res.results]


def kernel(**inputs):
    inp = {k: np.asarray(v) for k, v in inputs.items()}
    NT, SEG = 32, 16
    xp, xs = inp["x_prompt"], inp["x_sample"]
    mp, ms = inp["mem_prompt"], inp["mem_sample"]
    specs = []
    for b in range(4):
        specs.append((xp[b], np.stack([mp[b], mp[b]]), 1.0, np.arange(4096)))
    for c in range(4):
        specs.append((np.concatenate([xs[2 * c], xs[2 * c + 1]], 0), np.stack([ms[2 * c], ms[2 * c + 1]]), 0.0,
                      np.concatenate([np.arange(2048), np.arange(2048)])))
    ys = run_cores(inp, specs, NT, SEG)
    y_prompt = np.stack([ys[b] for b in range(4)]).astype(np.float32)
    y_sample = np.stack([ys[4 + c][sg * 2048:(sg + 1) * 2048] for c in range(4) for sg in range(2)]).astype(np.float32)
    return (y_prompt, y_sample)
```
